# Optimizing a Trainium2 kernel written in Bass

```python
import math
import jax, jax.numpy as jnp
from jax import lax
import numpy as np

D_MODEL = 2048
BATCH = 16
SEQ = 256
DEPTH = 2
DEC_BATCH = 8
DEC_SEQ = 1024
PAST_LEN = 256

GRID_W = 64
D_MIX = D_MODEL
D_SSM = D_MIX // 2
D_CONV = D_MIX - D_SSM
SSM_H = 16
SSM_GROUPS = D_SSM // SSM_H
SSM_STATE = 64
N_DIR = 2
CONV_W = 3
CONV_HEADS = 16
D_IN = D_SSM + 3 * D_CONV
D_FF = ((8 * D_MODEL + 3 * 256 - 1) // (3 * 256)) * 256
N_MOD = 6
EPS = 1e-6
LAM_RE_MAX = -1e-4
DT_MIN = 1e-3
DT_MAX = 1e-1

kernel_name = 'hymba_s5_shortconv_prefix_dit_step'


def rms_norm(x, g):
    xf = x.astype(jnp.float32)
    y = xf * lax.rsqrt(jnp.mean(xf * xf, axis=-1, keepdims=True) + EPS)
    return (y * g.astype(jnp.float32)).astype(x.dtype)


def modulation(cond, w, b):
    m = (jax.nn.silu(cond) @ w + b).reshape(cond.shape[0], N_MOD, 1, D_MODEL)
    return [m[:, i] for i in range(N_MOD)]


def _scan_combine(left, right):
    a_l, b_l = left
    a_r, b_r = right
    return a_r * a_l, a_r * b_l + b_r


def s5_direction(u, lam_re, lam_im, log_dt, b_re, b_im, c_re, c_im, s0, reverse):
    f32 = jnp.float32
    lam = lax.complex(jnp.minimum(lam_re.astype(f32), LAM_RE_MAX), lam_im.astype(f32))
    dt = jnp.exp(log_dt.astype(f32))[:, None]
    a_bar = jnp.exp(lam * dt)
    b_bar = ((a_bar - 1.0) / lam)[..., None] * lax.complex(b_re.astype(f32), b_im.astype(f32))
    bu = lax.complex(jnp.einsum('blgh,gph->blgp', u, b_bar.real),
                     jnp.einsum('blgh,gph->blgp', u, b_bar.imag))
    if s0 is not None:
        first = -1 if reverse else 0
        bu = bu.at[:, first].add(a_bar * s0)
    a = jnp.broadcast_to(a_bar, bu.shape)
    _, xs = lax.associative_scan(_scan_combine, (a, bu), axis=1, reverse=reverse)
    y = (jnp.einsum('blgp,ghp->blgh', xs.real, c_re.astype(f32))
         - jnp.einsum('blgp,ghp->blgh', xs.imag, c_im.astype(f32)))
    final = xs[:, 0] if reverse else xs[:, -1]
    return y, final


def conv3(z, w, b):
    zp = jnp.pad(z, [(0, 0)] * (z.ndim - 2) + [(1, 1), (0, 0)])
    return zp[..., :-2, :] * w[0] + zp[..., 1:-1, :] * w[1] + zp[..., 2:, :] * w[2] + b


def token_mixers(h, w_in, lam_re, lam_im, log_dt, b_re, b_im, c_re, c_im, d_skip, w_glu,
                 conv_w, conv_b, w_out, s0, on_grid):
    bsz, length, _ = h.shape
    z = h @ w_in
    u, gb, gc, v = jnp.split(z, [D_SSM, D_SSM + D_CONV, D_SSM + 2 * D_CONV], axis=-1)
    uf = u.astype(jnp.float32)
    ug = uf.reshape(bsz, length, SSM_GROUPS, SSM_H)
    ys, finals = [], []
    for d in range(N_DIR):
        y_d, f_d = s5_direction(ug, lam_re[d], lam_im[d], log_dt[d], b_re[d], b_im[d],
                                c_re[d], c_im[d], None if s0 is None else s0[d], d == 1)
        ys.append(y_d)
        finals.append(f_d)
    y = (ys[0] + ys[1]).reshape(bsz, length, D_SSM) + d_skip.astype(jnp.float32) * uf
    y = jax.nn.gelu(y).astype(h.dtype)
    y_ssm = y * jax.nn.sigmoid(y @ w_glu)
    zc = gc * v
    if on_grid:
        rows = length // GRID_W
        zc = conv3(zc.reshape(bsz, rows, GRID_W, D_CONV), conv_w, conv_b).reshape(bsz, length, D_CONV)
    else:
        zc = conv3(zc, conv_w, conv_b)
    y_conv = gb * zc
    out = jnp.concatenate([y_ssm, y_conv], axis=-1) @ w_out
    return out, finals


def setup_inputs(seed: int = 0) -> dict:
    key = jax.random.key(seed)
    ks = jax.random.split(key, 32)
    f32 = jnp.float32

    def nrm(k, shape, scale):
        return jax.random.normal(k, shape, f32) * scale

    gs = (DEPTH, N_DIR, SSM_GROUPS)
    lam_im_base = jnp.pi * jnp.arange(SSM_STATE, dtype=f32)
    return {
        'x_prompt': nrm(ks[0], (BATCH, SEQ, D_MODEL), 1.0),
        'x_sample': nrm(ks[1], (DEC_BATCH, DEC_SEQ, D_MODEL), 1.0),
        'state_ssm': nrm(ks[2], (DEC_BATCH, DEPTH, N_DIR, 2, SSM_GROUPS, SSM_STATE), 0.5),
        'c': nrm(ks[3], (DEC_BATCH, D_MODEL), 1.0),
        'c_ctx': nrm(ks[4], (D_MODEL,), 1.0),
        'w_ada': nrm(ks[5], (DEPTH, D_MODEL, N_MOD * D_MODEL), 0.5 * D_MODEL ** -0.5),
        'b_ada': nrm(ks[6], (DEPTH, N_MOD * D_MODEL), 0.02),
        'g_mix': 1.0 + nrm(ks[7], (DEPTH, D_MODEL), 0.02),
        'w_in': nrm(ks[8], (DEPTH, D_MODEL, D_IN), D_MODEL ** -0.5),
        'ssm_lam_re': -0.5 + nrm(ks[9], gs + (SSM_STATE,), 0.01),
        'ssm_lam_im': lam_im_base + nrm(ks[10], gs + (SSM_STATE,), 0.01),
        'ssm_log_dt': jax.random.uniform(ks[11], gs, f32, math.log(DT_MIN), math.log(DT_MAX)),
        'ssm_b_re': nrm(ks[12], gs + (SSM_STATE, SSM_H), (2.0 * SSM_H) ** -0.5),
        'ssm_b_im': nrm(ks[13], gs + (SSM_STATE, SSM_H), (2.0 * SSM_H) ** -0.5),
        'ssm_c_re': nrm(ks[14], gs + (SSM_H, SSM_STATE), (2.0 * SSM_STATE) ** -0.5),
        'ssm_c_im': nrm(ks[15], gs + (SSM_H, SSM_STATE), (2.0 * SSM_STATE) ** -0.5),
        'ssm_d': nrm(ks[16], (DEPTH, D_SSM), 0.5),
        'w_glu': nrm(ks[17], (DEPTH, D_SSM, D_SSM), D_SSM ** -0.5),
        'conv_w': nrm(ks[18], (DEPTH, CONV_W, D_CONV), CONV_W ** -0.5),
        'conv_b': nrm(ks[19], (DEPTH, D_CONV), 0.02),
        'w_out': nrm(ks[20], (DEPTH, D_MIX, D_MODEL), D_MIX ** -0.5),
        'g_ffn': 1.0 + nrm(ks[21], (DEPTH, D_MODEL), 0.02),
        'w_gate': nrm(ks[22], (DEPTH, D_MODEL, D_FF), D_MODEL ** -0.5),
        'w_up': nrm(ks[23], (DEPTH, D_MODEL, D_FF), D_MODEL ** -0.5),
        'w_down': nrm(ks[24], (DEPTH, D_FF, D_MODEL), D_FF ** -0.5),
        'g_final': 1.0 + nrm(ks[25], (D_MODEL,), 0.02),
    }


def reference(x_prompt, x_sample, state_ssm, c, c_ctx, w_ada, b_ada, g_mix, w_in,
              ssm_lam_re, ssm_lam_im, ssm_log_dt, ssm_b_re, ssm_b_im, ssm_c_re, ssm_c_im,
              ssm_d, w_glu, conv_w, conv_b, w_out, g_ffn, w_gate, w_up, w_down, g_final):

    def layer(x, cond, l, s0, on_grid):
        sh1, sc1, g1, sh2, sc2, g2 = modulation(cond, w_ada[l], b_ada[l])
        h = rms_norm(x, g_mix[l]) * (1.0 + sc1) + sh1
        out, finals = token_mixers(h, w_in[l], ssm_lam_re[l], ssm_lam_im[l], ssm_log_dt[l],
                                   ssm_b_re[l], ssm_b_im[l], ssm_c_re[l], ssm_c_im[l], ssm_d[l],
                                   w_glu[l], conv_w[l], conv_b[l], w_out[l], s0, on_grid)
        x = x + g1 * out
        h = rms_norm(x, g_ffn[l]) * (1.0 + sc2) + sh2
        x = x + g2 * ((jax.nn.silu(h @ w_gate[l]) * (h @ w_up[l])) @ w_down[l])
        return x, finals

    ctx_cond = c_ctx[None, :]
    xp = x_prompt
    per_layer_states = []
    for l in range(DEPTH):
        xp, finals = layer(xp, ctx_cond, l, None, False)
        per_layer_states.append(jnp.stack([jnp.stack([f.real, f.imag], axis=1) for f in finals], axis=1))
    y_prompt = rms_norm(xp, g_final)
    new_state_ssm = jnp.stack(per_layer_states, axis=1).astype(x_prompt.dtype)

    xs = x_sample
    for l in range(DEPTH):
        st = state_ssm[:, l].astype(jnp.float32)
        s0 = [lax.complex(st[:, d, 0], st[:, d, 1]) for d in range(N_DIR)]
        xs, _ = layer(xs, c, l, s0, True)
    y_sample = rms_norm(xs, g_final)

    return (y_prompt, y_sample, new_state_ssm)
```

```python
import numpy as np
import ml_dtypes
import concourse.bass as bass
import concourse.mybir as mybir
from concourse.bass_utils import run_bass_kernel_spmd

F32 = mybir.dt.float32
BF16 = mybir.dt.bfloat16
I32 = mybir.dt.int32
AF = mybir.ActivationFunctionType
ALU = mybir.AluOpType

D = 2048
NKC = 16
DFF = 5632
NFF = 44
DIN = 4096
NG = 64
PI = float(np.pi)
NST = 4


class StopBuild(Exception):
    pass


class Op:
    __slots__ = ("eng", "fn", "deps", "signals", "sigval", "is_dma", "dsem", "dval", "prev_dma", "name")

    def __init__(self, eng, fn, is_dma, name):
        self.eng = eng
        self.fn = fn
        self.deps = set()
        self.signals = False
        self.sigval = 0
        self.is_dma = is_dma
        self.dsem = None
        self.dval = 0
        self.prev_dma = None
        self.name = name


class Prog:
    ENG = ("pe", "act", "dve", "pool", "sp")

    def __init__(self, nc):
        self.nc = nc
        self.eng_ops = {e: [] for e in self.ENG}
        self.res_w = {}
        self.res_r = {}
        self.ctx = []
        self.engsem = {}
        for e in self.ENG:
            cm = nc.semaphore("s_" + e)
            self.engsem[e] = cm.__enter__()
            self.ctx.append(cm)
        n_dma_sems = {"sp": 8, "pool": 8}
        self.dsems = {}
        self.dcnt = {}
        self.dlast = {}
        for q, n in n_dma_sems.items():
            lst = []
            for i in range(n):
                cm = nc.semaphore("d_%s%d" % (q, i))
                lst.append(cm.__enter__())
                self.ctx.append(cm)
            self.dsems[q] = lst
            self.dcnt[q] = 0
            self.dlast[q] = [None] * n
        self.dvals = {}
        self.out_dmas = []

    def add(self, eng, fn, reads=(), writes=(), dma=False, name=""):
        op = Op(eng, fn, dma, name)
        deps = op.deps
        for r in reads:
            w = self.res_w.get(r)
            if w is not None:
                deps.add(w)
        for k in writes:
            w = self.res_w.get(k)
            if w is not None:
                deps.add(w)
            for rd in self.res_r.get(k, ()):
                deps.add(rd)
        for d in deps:
            d.signals = True
        for r in reads:
            lst = self.res_r.setdefault(r, [])
            if not dma:
                lst[:] = [o for o in lst if o.is_dma or o.eng != eng]
            lst.append(op)
        for k in writes:
            self.res_w[k] = op
            self.res_r[k] = []
        if dma:
            q = eng
            i = self.dcnt[q] % len(self.dsems[q])
            self.dcnt[q] += 1
            op.dsem = self.dsems[q][i]
            key = (q, i)
            self.dvals[key] = self.dvals.get(key, 0) + 16
            op.dval = self.dvals[key]
            op.prev_dma = self.dlast[q][i]
            self.dlast[q][i] = op
        self.eng_ops[eng].append(op)
        return op

    def dma(self, q, out, in_, reads=(), writes=(), name="", is_output=False):
        op = self.add(q, lambda e: e.dma_start(out=out, in_=in_), reads, writes, dma=True, name=name)
        if is_output:
            self.out_dmas.append(op)
        return op

    def emit(self):
        fin = Op("sp", lambda e: None, False, "final")
        fin.deps = set(self.out_dmas)
        self.eng_ops["sp"].append(fin)
        for e in self.ENG:
            c = 0
            for op in self.eng_ops[e]:
                if (not op.is_dma) and op.signals:
                    c += 1
                    op.sigval = c
        nc = self.nc
        with nc.Block() as block:
            @block.tensor
            def _(eng):
                self._emit_engine("pe", eng)

            @block.scalar
            def _(eng):
                self._emit_engine("act", eng)

            @block.vector
            def _(eng):
                self._emit_engine("dve", eng)

            @block.gpsimd
            def _(eng):
                self._emit_engine("pool", eng)

            @block.sync
            def _(eng):
                self._emit_engine("sp", eng)
        for cm in reversed(self.ctx):
            cm.__exit__(None, None, None)

    def _emit_engine(self, e, eng):
        waited = {}
        for op in self.eng_ops[e]:
            needs = {}
            for d in op.deps:
                if d.is_dma:
                    s, v = d.dsem, d.dval
                else:
                    if d.eng == e and e == "pe":
                        continue
                    s, v = self.engsem[d.eng], d.sigval
                k = id(s)
                if k not in needs or needs[k][1] < v:
                    needs[k] = (s, v)
            if op.is_dma and op.prev_dma is not None:
                s, v = op.prev_dma.dsem, op.prev_dma.dval
                k = id(s)
                if k not in needs or needs[k][1] < v:
                    needs[k] = (s, v)
            for k, (s, v) in needs.items():
                if waited.get(k, 0) < v:
                    eng.wait_ge(s, v)
                    waited[k] = v
            ins = op.fn(eng)
            if op.is_dma:
                ins.then_inc(op.dsem, 16)
            elif op.signals:
                ins.then_inc(self.engsem[e], 1)


def build_nc(debug=False, stop=None):
    nc = bass.Bass("TRN2", target_bir_lowering=False)

    def din(name, shape, dt=F32):
        return nc.dram_tensor(name, list(shape), dt, kind="ExternalInput").ap()

    def dout(name, shape, dt=F32):
        return nc.dram_tensor(name, list(shape), dt, kind="ExternalOutput").ap()

    xP = din("xP", [D, 512])
    xS = din("xS", [D, 1024])
    condT_d = din("condT", [128, 32])
    sinit_d = din("sinit", [128, 256])
    w_ada = din("w_ada", [2, D, 6 * D])
    b_adaT_d = din("b_adaT", [128, 192])
    gmixT_d = din("gmixT", [128, 32])
    gffnT_d = din("gffnT", [128, 32])
    gfinT_d = din("gfinT", [128, 16])
    w_in = din("w_in", [2, D, DIN])
    w_glu = din("w_glu", [2, 1024, 1024])
    w_out = din("w_out", [2, D, D])
    w_gate = din("w_gate", [2, D, DFF])
    w_up = din("w_up", [2, D, DFF])
    w_down = din("w_down", [2, DFF, D])
    convw_d = din("convw", [128, 48])
    convb_d = din("convb", [128, 16])
    dtab_d = din("dtab", [128, 128])
    lamre_d = din("lamre", [128, 128])
    lamim_d = din("lamim", [128, 128])
    logdt_d = din("logdt", [128, 128])
    Bre_d = din("Bre_in", [128, 2048])
    Bim_d = din("Bim_in", [128, 2048])
    Cre_d = din("Cre_in", [128, 2048])
    Cim_d = din("Cim_in", [128, 2048])
    cf32_d = din("cF32", [128, 384])
    selA_d = din("selA_f", [128, 4096])
    yP = dout("yP", [D, 512])
    yS = dout("yS", [D, 1024])
    nst_d = dout("nst", [128, 512])
    ssmw_d = nc.dram_tensor("ssmw_scratch", [2, 8, 128, 5120], BF16, kind="Internal").ap()
    NWT = 384
    wbf_d = nc.dram_tensor("wbf_scratch", [NWT, 128, 2048], BF16, kind="Internal").ap()

    P = Prog(nc)
    cms = []

    def sb(name, shape, dt):
        cm = nc.sbuf_tensor(name, list(shape), dt)
        t = cm.__enter__()
        cms.append(cm)
        return t

    xT = sb("xT", [128, NKC, 1024], F32)
    hT = sb("hT", [128, NKC, 1024], BF16)
    mix = sb("mix", [128, NKC, 1024], BF16)
    wst = [sb("wst%d" % i, [128, 2048], BF16) for i in range(NST)]
    identf = sb("identf", [128, 128], F32)
    maskf = sb("maskf", [128, 128], F32)
    maskb = sb("maskb", [128, 128], F32)
    selA = sb("selA", [128, 4, 8, 128], BF16)
    ones_bf = sb("ones_bf", [128, 128], BF16)
    epsc = sb("epsc", [128, 1], F32)
    condT = sb("condT_sb", [128, 16, 2], F32)
    scond = sb("scond", [128, 16, 2], BF16)
    modT = sb("modT", [128, 2, 96, 2], F32)
    Atab = sb("Atab", [128, 2, 2, 16, 2], F32)
    b_adaT = sb("b_adaT_sb", [128, 2, 96], F32)
    gmixT = sb("gmixT_sb", [128, 2, 16], F32)
    gffnT = sb("gffnT_sb", [128, 2, 16], F32)
    gfinT = sb("gfinT_sb", [128, 16], F32)
    convw = sb("convw_sb", [128, 2, 3, 8], F32)
    convb = sb("convb_sb", [128, 2, 8], F32)
    sinit = sb("sinit_sb", [128, 2, 64, 2], F32)
    a8tab = sb("a8tab", [128, 2, 64, 4], F32)
    a64tab = sb("a64tab", [128, 2, 64, 4], F32)
    nst_sb = sb("nst_sb", [128, 2, 2, 64, 2], F32)
    u_sb = sb("u_sb", [128, 1024], BF16)
    Ush = sb("Ush", [128, 2, 16, 128], BF16)
    carry = sb("carry", [128, 640], F32)
    Xbf = sb("Xbf", [128, 8, 2, 128], BF16)
    Ysb = sb("Ysb", [128, 8, 128], BF16)
    ssmwB = sb("ssmwB", [128, 8, 2, 128], BF16)
    ssmwC = sb("ssmwC", [128, 8, 3, 128], BF16)
    gcs = sb("gcs", [128, 1024], F32)
    zc = sb("zc", [128, 1024], F32)
    acc = gcs
    sq = [sb("sq%d" % i, [128, 512], BF16) for i in range(2)]
    rstd = zc
    tmpf = [sb("tmpf%d" % i, [128, 512], F32) for i in range(2)]

    pcm = nc.psum_tensor("psum_all", [128, 4096], F32)
    psum = pcm.__enter__()
    cms.append(pcm)

    def bank(b):
        return psum[:, b * 512:(b + 1) * 512]

    counters = {"ws": 0, "sq": 0, "tmp": 0}

    P.dma("sp", identf[:], cf32_d[:, 0:128], writes=["identf"])
    P.dma("sp", maskf[:], cf32_d[:, 128:256], writes=["maskf"])
    P.dma("sp", maskb[:], cf32_d[:, 256:384], writes=["maskb"])
    P.dma("pool", selA[:].rearrange("p a b c -> p (a b c)"), selA_d, writes=["selA"])
    P.add("dve", lambda e: e.memset(ones_bf[:], 1.0), writes=["ones_bf"])
    P.add("dve", lambda e: e.memset(epsc[:], 1e-6), writes=["epsc"])
    P.dma("sp", condT[:].rearrange("p a b -> p (a b)"), condT_d, writes=["condT"])
    P.dma("sp", b_adaT[:].rearrange("p a b -> p (a b)"), b_adaT_d, writes=["b_adaT"])
    P.dma("sp", gmixT[:].rearrange("p a b -> p (a b)"), gmixT_d, writes=["gmixT"])
    P.dma("sp", gffnT[:].rearrange("p a b -> p (a b)"), gffnT_d, writes=["gffnT"])
    P.dma("sp", gfinT[:], gfinT_d, writes=["gfinT"])
    P.dma("sp", convw[:].rearrange("p a b c -> p (a b c)"), convw_d, writes=["convw"])
    P.dma("sp", convb[:].rearrange("p a b -> p (a b)"), convb_d, writes=["convb"])
    P.dma("sp", sinit[:].rearrange("p a b c -> p (a b c)"), sinit_d, writes=["sinit"])
    P.add("act", lambda e: e.activation(out=scond[:], in_=condT[:], func=AF.Silu), reads=["condT"], writes=["scond"])

    xflat = xT[:].rearrange("p a b -> p (a b)")
    hflat = hT[:].rearrange("p a b -> p (a b)").bitcast(F32)

    class Carve:
        def __init__(self, flat):
            self.flat = flat
            self.off = 0

        def get(self, n):
            v = self.flat[:, self.off:self.off + n]
            self.off += n
            return v

    cx = Carve(xflat)
    lamre = cx.get(128)
    lamim = cx.get(128)
    logdt = cx.get(128)
    dtab = cx.get(128)
    Bre_in = cx.get(1024)
    Bim_in = cx.get(1024)
    Cre_in = cx.get(1024)
    Cim_in = cx.get(1024)
    P.dma("sp", lamre, lamre_d, writes=["lamre"])
    P.dma("sp", lamim, lamim_d, writes=["lamim"])
    P.dma("sp", logdt, logdt_d, writes=["logdt"])
    P.dma("sp", dtab, dtab_d, writes=["dtab"])
    pro_keys = ["lamre", "lamim", "logdt", "dtab", "Bre_in", "Bim_in", "Cre_in", "Cim_in"]

    def tab(name, n):
        pro_keys.append(name)
        return cx.get(n)

    dtt = tab("dtt", 128)
    lr = tab("lr", 128)
    lrdt = tab("lrdt", 128)
    th = tab("th", 128)
    kf = tab("kf", 128)
    ki_f = tab("ki", 128)
    ki = ki_f.bitcast(I32)
    ang = tab("ang", 128)
    fix = tab("fix", 128)
    sin1 = tab("sin1", 128)
    cos1 = tab("cos1", 128)
    mag1 = tab("mag1", 128)
    crk = tab("crk", 9 * 128)
    cik = tab("cik", 9 * 128)
    mN = tab("mN", 128)
    crN = tab("crN", 8 * 128)
    ciN = tab("ciN", 8 * 128)
    ta = tab("ta", 128)
    tb = tab("tb", 128)
    den = tab("den", 128)
    wr = tab("wr", 128)
    wi = tab("wi", 128)
    crB = tab("crB", 512)
    ciB = tab("ciB", 512)
    crC = tab("crC", 512)
    ciC = tab("ciC", 512)
    crR = tab("crR", 512)
    ciR = tab("ciR", 512)
    assert cx.off <= 16384, cx.off
    ch = Carve(hflat)
    bbr = ch.get(128)
    bbi = ch.get(128)
    BTre = ch.get(1024)
    BTim = ch.get(1024)
    Rre = ch.get(1024)
    Rimn = ch.get(1024)
    CTre = ch.get(1024)
    CTimn = ch.get(1024)
    tq = gcs[:]
    tq2 = zc[:]
    assert ch.off <= 8192
    stg_flat = mix[:].rearrange("p a b -> p (a b)")
    stg = [stg_flat[:, i * 5120:(i + 1) * 5120] for i in range(2)]
    def dv(fn, reads, writes, eng="dve"):
        return P.add(eng, fn, reads=reads, writes=writes)

    def tt_op(eng, out, a, b, op, reads, writes):
        return P.add(eng, lambda e: e.tensor_tensor(out=out, in0=a, in1=b, op=op), reads=reads, writes=writes)

    P.add("act", lambda e: e.activation(out=dtt, in_=logdt, func=AF.Exp), reads=["logdt"], writes=["dtt"])
    dv(lambda e: e.tensor_single_scalar(out=lr, in_=lamre, scalar=-1e-4, op=ALU.min), ["lamre"], ["lr"])
    tt_op("dve", lrdt, lr, dtt, ALU.mult, ["lr", "dtt"], ["lrdt"])
    tt_op("dve", th, lamim, dtt, ALU.mult, ["lamim", "dtt"], ["th"])

    def range_reduce_sin(dst, shift, name):
        dv(lambda e: e.tensor_scalar(out=kf, in0=th, scalar1=shift, scalar2=1.0 / (2 * PI), op0=ALU.add, op1=ALU.mult), ["th"], ["kf"])
        dv(lambda e: e.tensor_copy(out=ki, in_=kf), ["kf"], ["ki"])
        dv(lambda e: e.tensor_copy(out=kf, in_=ki), ["ki"], ["kf"])
        dv(lambda e: e.tensor_scalar(out=kf, in0=kf, scalar1=-2 * PI, scalar2=shift, op0=ALU.mult, op1=ALU.add), ["kf"], ["kf"])
        tt_op("dve", ang, th, kf, ALU.add, ["th", "kf"], ["ang"])
        dv(lambda e: e.tensor_scalar(out=fix, in0=ang, scalar1=PI, scalar2=-2 * PI, op0=ALU.is_gt, op1=ALU.mult), ["ang"], ["fix"])
        tt_op("dve", ang, ang, fix, ALU.add, ["ang", "fix"], ["ang"])
        dv(lambda e: e.tensor_scalar(out=fix, in0=ang, scalar1=-PI, scalar2=2 * PI, op0=ALU.is_lt, op1=ALU.mult), ["ang"], ["fix"])
        tt_op("dve", ang, ang, fix, ALU.add, ["ang", "fix"], ["ang"])
        P.add("act", lambda e: e.activation(out=dst, in_=ang, func=AF.Sin), reads=["ang"], writes=[name])

    range_reduce_sin(sin1, 0.0, "sin1")
    range_reduce_sin(cos1, PI / 2, "cos1")
    P.add("act", lambda e: e.activation(out=mag1, in_=lrdt, func=AF.Exp), reads=["lrdt"], writes=["mag1"])

    def K(tabv, k):
        return tabv[:, k * 128:(k + 1) * 128]

    dv(lambda e: e.memset(K(crk, 0), 1.0), [], ["crk"])
    dv(lambda e: e.memset(K(cik, 0), 0.0), [], ["cik"])
    tt_op("dve", K(crk, 1), mag1, cos1, ALU.mult, ["mag1", "cos1", "crk"], ["crk"])
    tt_op("dve", K(cik, 1), mag1, sin1, ALU.mult, ["mag1", "sin1", "cik"], ["cik"])
    for k in range(2, 9):
        tt_op("dve", ta, K(crk, k - 1), K(crk, 1), ALU.mult, ["crk"], ["ta"])
        tt_op("dve", tb, K(cik, k - 1), K(cik, 1), ALU.mult, ["cik"], ["tb"])
        tt_op("dve", K(crk, k), ta, tb, ALU.subtract, ["ta", "tb", "crk"], ["crk"])
        tt_op("dve", ta, K(crk, k - 1), K(cik, 1), ALU.mult, ["crk", "cik"], ["ta"])
        tt_op("dve", tb, K(cik, k - 1), K(crk, 1), ALU.mult, ["crk", "cik"], ["tb"])
        tt_op("dve", K(cik, k), ta, tb, ALU.add, ["ta", "tb", "cik"], ["cik"])
    for k in range(8):
        P.add("act", (lambda k: lambda e: e.activation(out=mN, in_=lrdt, func=AF.Exp, scale=-2.0 * k))(k), reads=["lrdt", "crN", "ciN"], writes=["mN"])
        tt_op("dve", K(crN, k), K(crk, k), mN, ALU.mult, ["crk", "mN"], ["crN"])
        dv((lambda k: lambda e: e.scalar_tensor_tensor(out=K(ciN, k), in0=K(cik, k), scalar=-1.0, in1=mN, op0=ALU.mult, op1=ALU.mult))(k), ["cik", "mN"], ["ciN"])
    tt_op("dve", ta, lr, lr, ALU.mult, ["lr"], ["ta"])
    tt_op("dve", tb, lamim, lamim, ALU.mult, ["lamim"], ["tb"])
    tt_op("dve", den, ta, tb, ALU.add, ["ta", "tb"], ["den"])
    dv(lambda e: e.reciprocal(out=den, in_=den), ["den"], ["den"])
    dv(lambda e: e.tensor_scalar(out=ta, in0=K(crk, 1), scalar1=-1.0, scalar2=None, op0=ALU.add), ["crk"], ["ta"])
    tt_op("dve", tb, ta, lr, ALU.mult, ["ta", "lr"], ["tb"])
    tt_op("dve", wr, K(cik, 1), lamim, ALU.mult, ["cik", "lamim"], ["wr"])
    tt_op("dve", wr, wr, tb, ALU.add, ["wr", "tb"], ["wr"])
    tt_op("dve", wr, wr, den, ALU.mult, ["wr", "den"], ["wr"])
    tt_op("dve", tb, ta, lamim, ALU.mult, ["ta", "lamim"], ["tb"])
    tt_op("dve", wi, K(cik, 1), lr, ALU.mult, ["cik", "lr"], ["wi"])
    tt_op("dve", wi, wi, tb, ALU.subtract, ["wi", "tb"], ["wi"])
    tt_op("dve", wi, wi, den, ALU.mult, ["wi", "den"], ["wi"])
    a8v = a8tab[:].rearrange("p l g f -> p (l g) f")
    dv(lambda e: e.tensor_copy(out=a8v[:, :, 0], in_=K(crk, 8)), ["crk"], ["a8tab"])
    dv(lambda e: e.tensor_copy(out=a8v[:, :, 1], in_=K(crk, 8)), ["crk", "a8tab"], ["a8tab"])
    dv(lambda e: e.tensor_scalar(out=a8v[:, :, 2], in0=K(cik, 8), scalar1=-1.0, scalar2=None, op0=ALU.mult), ["cik", "a8tab"], ["a8tab"])
    dv(lambda e: e.tensor_copy(out=a8v[:, :, 3], in_=K(cik, 8)), ["cik", "a8tab"], ["a8tab"])

    sre, sim_ = tab("sre", 128), tab("sim", 128)
    dv(lambda e: e.tensor_copy(out=sre, in_=K(crk, 8)), ["crk"], ["sre"])
    dv(lambda e: e.tensor_copy(out=sim_, in_=K(cik, 8)), ["cik"], ["sim"])
    for _ in range(3):
        tt_op("dve", ta, sre, sre, ALU.mult, ["sre"], ["ta"])
        tt_op("dve", tb, sim_, sim_, ALU.mult, ["sim"], ["tb"])
        dv(lambda e: e.scalar_tensor_tensor(out=sim_, in0=sre, scalar=2.0, in1=sim_, op0=ALU.mult, op1=ALU.mult), ["sre", "sim"], ["sim"])
        tt_op("dve", sre, ta, tb, ALU.subtract, ["ta", "tb", "sim"], ["sre"])
    a64v = a64tab[:].rearrange("p l g f -> p (l g) f")
    dv(lambda e: e.tensor_copy(out=a64v[:, :, 0], in_=sre), ["sre"], ["a64tab"])
    dv(lambda e: e.tensor_copy(out=a64v[:, :, 1], in_=sre), ["sre", "a64tab"], ["a64tab"])
    dv(lambda e: e.tensor_scalar(out=a64v[:, :, 2], in0=sim_, scalar1=-1.0, scalar2=None, op0=ALU.mult), ["sim", "a64tab"], ["a64tab"])
    dv(lambda e: e.tensor_copy(out=a64v[:, :, 3], in_=sim_), ["sim", "a64tab"], ["a64tab"])

    def k4(tabv, nk):
        return tabv[:, 0:nk * 128].rearrange("p (k l g) -> p k l g", k=nk, l=2)

    crk4, cik4, crN4, ciN4 = k4(crk, 9), k4(cik, 9), k4(crN, 8), k4(ciN, 8)

    def idx_tab(dst, src4, lo_sl, hi_sl, rd, wrk, l):
        d3 = dst.rearrange("p (k g) -> p k g", k=8)
        dv(lambda e: e.tensor_copy(out=d3[0:64], in_=src4[0:64, lo_sl, l, :]), [rd, wrk], [wrk])
        dv(lambda e: e.tensor_copy(out=d3[64:128], in_=src4[64:128, hi_sl, l, :]), [rd, wrk], [wrk])

    rev7 = slice(7, None, -1)

    hkeys = ["bbr", "bbi", "BTre", "BTim", "Rre", "Rimn", "CTre", "CTimn", "tq", "tq2", "gcs", "zc"]
    pro_keys += hkeys

    def bc_gh(v, l, mu, width):
        base = (mu * 8) * width
        return v[:, base:base + 8 * width].rearrange("p (g h) -> p g h", g=8).unsqueeze(2).to_broadcast([128, 8, 8, width])

    def bc_kg(v, l, mu):
        vv = v.rearrange("p (k g) -> p g k", k=8)[:, mu * 8:(mu + 1) * 8, :]
        return vv.unsqueeze(3).to_broadcast([128, 8, 8, 16])

    def v4(v):
        return v.rearrange("p (g k h) -> p g k h", g=8, k=8)

    def cplx_tab(dre, dim_, are, aim, l, mu, cr_t, ci_t, neg_im, tagre, tagim, e1, e2):
        A_re, A_im = are, aim
        tt_op(e1, v4(dre), A_re, bc_kg(cr_t, l, mu), ALU.mult, ["Bre_in", "Bim_in", "Cre_in", "Cim_in", "bbr", "bbi", "crB", "crC", "crR"], [tagre])
        tt_op(e1, v4(tq), A_im, bc_kg(ci_t, l, mu), ALU.mult, ["Bre_in", "Bim_in", "Cre_in", "Cim_in", "bbr", "bbi", "ciB", "ciC", "ciR"], ["tq"])
        tt_op(e1, dre, dre, tq, ALU.subtract, [tagre, "tq"], [tagre])
        tt_op(e2, v4(dim_), A_re, bc_kg(ci_t, l, mu), ALU.mult, ["Bre_in", "Bim_in", "Cre_in", "Cim_in", "bbr", "bbi", "ciB", "ciC", "ciR"], [tagim])
        tt_op(e2, v4(tq2), A_im, bc_kg(cr_t, l, mu), ALU.mult, ["Bre_in", "Bim_in", "Cre_in", "Cim_in", "bbr", "bbi", "crB", "crC", "crR"], ["tq2"])
        if neg_im:
            P.add("dve", lambda e: e.scalar_tensor_tensor(out=dim_, in0=dim_, scalar=-1.0, in1=tq2, op0=ALU.mult, op1=ALU.subtract),
                  reads=[tagim, "tq2"], writes=[tagim])
        else:
            tt_op(e2, dim_, dim_, tq2, ALU.add, [tagim, "tq2"], [tagim])

    wmode = {"mode": "plain", "tid": 0}

    def wload(src3, nk, width=128):
        s = counters["ws"] % NST
        counters["ws"] += 1
        dst = wst[s][:, 0:nk * width].rearrange("p (k c) -> p k c", k=nk)
        if wmode["mode"] == "reuse":
            tid = wmode["tid"]
            wmode["tid"] += 1
            P.dma("pool", wst[s][:, 0:nk * width], wbf_d[tid][:, 0:nk * width], reads=[("wbf", tid)], writes=[("ws", s)])
            return s, dst
        P.dma("pool", dst, src3, writes=[("ws", s)])
        if wmode["mode"] == "save":
            tid = wmode["tid"]
            wmode["tid"] += 1
            P.dma("sp", wbf_d[tid][:, 0:nk * width], wst[s][:, 0:nk * width], reads=[("ws", s)], writes=[("wbf", tid)])
        return s, dst

    MODBANK = {0: 4, 1: 5}

    def mod_tile(l, m):
        wav = w_ada[l].rearrange("(k p) c -> p k c", p=128)
        psM = bank(MODBANK[l])
        s, wv = wload(wav[:, :, m * 128:(m + 1) * 128], 16)

        def mm(e, wv=wv, m=m, psM=psM):
            ins = None
            for kc in range(NKC):
                ins = e.matmul(psM[:, m * 2:m * 2 + 2], lhsT=wv[:, kc, :], rhs=scond[:, kc, :], start=(kc == 0), stop=(kc == NKC - 1))
            return ins
        P.add("pe", mm, reads=[("ws", s), "scond"], writes=[("ps", MODBANK[l])])

    def mod_finish(l):
        psM = bank(MODBANK[l])
        tt_op("dve", modT[:, l], psM[:, 0:192].rearrange("p (m g) -> p m g", g=2),
              b_adaT[:, l].unsqueeze(2).to_broadcast([128, 96, 2]), ALU.add, [("ps", MODBANK[l]), "b_adaT"], ["modT"])
        for n, (gT, gk, scoff) in enumerate(((gmixT, "gmixT", 16), (gffnT, "gffnT", 64))):
            P.add("dve", lambda e, l=l, n=n, gT=gT, scoff=scoff: e.scalar_tensor_tensor(
                out=Atab[:, l, n], in0=modT[:, l, scoff:scoff + 16, :], scalar=1.0,
                in1=gT[:, l].unsqueeze(2).to_broadcast([128, 16, 2]), op0=ALU.add, op1=ALU.mult),
                reads=["modT", gk], writes=["Atab"])

    mod_list = [(l, m) for l in range(2) for m in range(96)]

    for l in range(2):
        P.dma("sp", Bre_in, Bre_d[:, l * 1024:(l + 1) * 1024], writes=["Bre_in"])
        P.dma("sp", Bim_in, Bim_d[:, l * 1024:(l + 1) * 1024], writes=["Bim_in"])
        P.dma("sp", Cre_in, Cre_d[:, l * 1024:(l + 1) * 1024], writes=["Cre_in"])
        P.dma("sp", Cim_in, Cim_d[:, l * 1024:(l + 1) * 1024], writes=["Cim_in"])
        idx_tab(crB, crk4, rev7, slice(0, 8), "crk", "crB", l)
        idx_tab(ciB, cik4, rev7, slice(0, 8), "cik", "ciB", l)
        idx_tab(crC, crk4, slice(1, 9), slice(8, 0, -1), "crk", "crC", l)
        idx_tab(ciC, cik4, slice(1, 9), slice(8, 0, -1), "cik", "ciC", l)
        idx_tab(crR, crN4, rev7, slice(0, 8), "crN", "crR", l)
        idx_tab(ciR, ciN4, rev7, slice(0, 8), "ciN", "ciR", l)
        for mu in range(8):
            it = l * 8 + mu
            sg = stg[it % 2]
            sgk = "stg%d" % (it % 2)
            sg5 = sg.rearrange("p (g s c) -> p g s c", g=8, s=5)
            b0 = (l * 64 + mu * 8)
            wrb = wr[:, b0:b0 + 8].unsqueeze(2).to_broadcast([128, 8, 16])
            wib = wi[:, b0:b0 + 8].unsqueeze(2).to_broadcast([128, 8, 16])
            Br = Bre_in[:, mu * 128:(mu + 1) * 128].rearrange("p (g h) -> p g h", g=8)
            Bi = Bim_in[:, mu * 128:(mu + 1) * 128].rearrange("p (g h) -> p g h", g=8)
            bbr3 = bbr.rearrange("p (g h) -> p g h", g=8)
            bbi3 = bbi.rearrange("p (g h) -> p g h", g=8)
            tq3 = tq[:, 0:128].rearrange("p (g h) -> p g h", g=8)
            tt_op("dve", bbr3, Br, wrb, ALU.mult, ["Bre_in", "wr", "BTre", "BTim"], ["bbr"])
            tt_op("dve", tq3, Bi, wib, ALU.mult, ["Bim_in", "wi"], ["tq"])
            tt_op("dve", bbr3, bbr3, tq3, ALU.subtract, ["bbr", "tq"], ["bbr"])
            tt_op("dve", bbi3, Bi, wrb, ALU.mult, ["Bim_in", "wr", "BTre", "BTim"], ["bbi"])
            tt_op("dve", tq3, Br, wib, ALU.mult, ["Bre_in", "wi"], ["tq"])
            tt_op("dve", bbi3, bbi3, tq3, ALU.add, ["bbi", "tq"], ["bbi"])
            bbr_b = bbr3.unsqueeze(2).to_broadcast([128, 8, 8, 16])
            bbi_b = bbi3.unsqueeze(2).to_broadcast([128, 8, 8, 16])
            cplx_tab(BTre, BTim, bbr_b, bbi_b, l, mu, crB, ciB, False, "BTre", "BTim", "dve", "dve")
            cplx_tab(Rre, Rimn, bc_gh(Cre_in, l, mu, 16), bc_gh(Cim_in, l, mu, 16), l, mu, crR, ciR, True, "Rre", "Rimn", "dve", "dve")
            cplx_tab(CTre, CTimn, bc_gh(Cre_in, l, mu, 16), bc_gh(Cim_in, l, mu, 16), l, mu, crC, ciC, True, "CTre", "CTimn", "dve", "dve")
            P.add("act", lambda e, sg5=sg5: e.copy(out=sg5[:, :, 3, :], in_=CTre.rearrange("p (g c) -> p g c", g=8)),
                  reads=["CTre"], writes=[sgk])
            P.add("act", lambda e, sg5=sg5: e.copy(out=sg5[:, :, 4, :], in_=CTimn.rearrange("p (g c) -> p g c", g=8)),
                  reads=["CTimn", sgk], writes=[sgk])
            for half in range(2):
                gsl = slice(half * 4, half * 4 + 4)
                def tr(e, half=half):
                    ins = None
                    for gi in range(4):
                        g8 = half * 4 + gi
                        e.transpose(out=psum[:, (gi * 2) * 128:(gi * 2 + 1) * 128], in_=BTre[:, g8 * 128:(g8 + 1) * 128], identity=identf[:])
                        ins = e.transpose(out=psum[:, (gi * 2 + 1) * 128:(gi * 2 + 2) * 128], in_=BTim[:, g8 * 128:(g8 + 1) * 128], identity=identf[:])
                    return ins
                P.add("pe", tr, reads=["BTre", "BTim", "identf"], writes=[("ps", 0), ("ps", 1)])
                for bb in range(2):
                    g0_ = half * 4 + bb * 2
                    P.add("act", lambda e, sg5=sg5, g0_=g0_, bb=bb: e.copy(out=sg5[:, g0_:g0_ + 2, 0:2, :],
                                                                        in_=bank(bb).rearrange("p (g s c) -> p g s c", g=2, s=2)),
                          reads=[("ps", bb), sgk], writes=[sgk])
                def tp(e, half=half):
                    ins = None
                    for gi in range(4):
                        g8 = half * 4 + gi
                        cs = slice(g8 * 128, (g8 + 1) * 128)
                        for d in range(2):
                            ps_ = psum[:, (2 + d) * 512 + gi * 128:(2 + d) * 512 + (gi + 1) * 128]
                            rows = slice(d * 64, d * 64 + 64)
                            e.matmul(ps_, lhsT=BTre[rows, cs], rhs=Rre[rows, cs], start=True, stop=False)
                            ins = e.matmul(ps_, lhsT=BTim[rows, cs], rhs=Rimn[rows, cs], start=False, stop=True)
                    return ins
                P.add("pe", tp, reads=["BTre", "BTim", "Rre", "Rimn"], writes=[("ps", 2), ("ps", 3)])
                mf_b = maskf[:].unsqueeze(1).to_broadcast([128, 4, 128])
                mb_b = maskb[:].unsqueeze(1).to_broadcast([128, 4, 128])
                id_b = identf[:].unsqueeze(1).to_broadcast([128, 4, 128])
                tqa = tq[:, 0:512].rearrange("p (g c) -> p g c", g=4)
                tqb = tq2[:, 0:512].rearrange("p (g c) -> p g c", g=4)
                tt_op("dve", tqa, bank(2).rearrange("p (g c) -> p g c", g=4), mf_b, ALU.mult, [("ps", 2), "maskf"], ["tq"])
                tt_op("dve", tqb, bank(3).rearrange("p (g c) -> p g c", g=4), mb_b, ALU.mult, [("ps", 3), "maskb"], ["tq2"])
                tt_op("dve", tqa, tqa, tqb, ALU.add, ["tq", "tq2"], ["tq"])
                dcol = dtab[:, b0 + half * 4:b0 + half * 4 + 4].unsqueeze(2).to_broadcast([128, 4, 128])
                tt_op("dve", tqb, id_b, dcol, ALU.mult, ["identf", "dtab"], ["tq2"])
                tt_op("dve", sg5[:, gsl, 2, :], tqa, tqb, ALU.add, ["tq", "tq2", sgk], [sgk])
            P.dma("sp", ssmw_d[l, mu], sg, reads=[sgk], writes=[("ssmw_d", l, mu)])
            for (ml_, mm_) in mod_list[it * 12:(it + 1) * 12]:
                mod_tile(ml_, mm_)
    mod_finish(0)
    mod_finish(1)

    main_keys = [("xT", kc, tt) for kc in range(NKC) for tt in range(2)] + [("hT", kc, tt) for kc in range(NKC) for tt in range(2)] + \
                [("mix", j, tt) for j in range(NKC) for tt in range(2)]
    dummy = sb("barrier_dummy", [128, 2], F32)
    P.add("dve", lambda e: e.memset(dummy[:], 0.0), reads=pro_keys + ["stg0", "stg1"], writes=main_keys + ["gcs", "zc", "acc", ("rstd", 0), ("rstd", 1)])

    def run_pass(grp, NT, x_d, y_d, seqs):
        TT = NT // 512
        NC = NT // 8
        nseq = len(seqs)
        NCs = NC // nseq
        SB = NCs + 1
        tts = list(range(TT))

        def xk(kc, tt):
            return ("xT", kc, tt)

        def hk(kc, tt):
            return ("hT", kc, tt)

        def mk(j, tt):
            return ("mix", j, tt)

        def tsl(tt):
            return slice(tt * 512, (tt + 1) * 512)

        xv = x_d.rearrange("(k p) t -> p k t", p=128)
        for kc in range(NKC):
            P.dma("sp", xT[:, kc, 0:NT], xv[:, kc, :], writes=[xk(kc, tt) for tt in tts])

        def norm(l, n, final=False):
            for tt in tts:
                for kc in range(NKC):
                    si = counters["sq"] % 2
                    counters["sq"] += 1
                    P.add("act", lambda e, si=si, kc=kc, tt=tt: e.activation(out=sq[si][:], in_=xT[:, kc, tsl(tt)], func=AF.Square),
                          reads=[xk(kc, tt)], writes=[("sq", si)])
                    P.add("pe", lambda e, si=si, kc=kc, tt=tt: e.matmul(bank(4 + tt), lhsT=ones_bf[:], rhs=sq[si][:], start=(kc == 0), stop=(kc == NKC - 1)),
                          reads=[("sq", si), "ones_bf"], writes=[("ps", 4 + tt)])
                P.add("act", lambda e, tt=tt: e.activation(out=rstd[:, tsl(tt)], in_=bank(4 + tt), func=AF.Sqrt, bias=epsc[:, 0:1], scale=1.0 / D),
                      reads=[("ps", 4 + tt), "epsc", "zc"], writes=[("rstd", tt), "zc"])
                P.add("dve", lambda e, tt=tt: e.reciprocal(out=rstd[:, tsl(tt)], in_=rstd[:, tsl(tt)]), reads=[("rstd", tt), "zc"], writes=[("rstd", tt), "zc"])
                for kc in range(NKC):
                    ti = counters["tmp"] % 2
                    counters["tmp"] += 1
                    P.add("dve", lambda e, ti=ti, kc=kc, tt=tt: e.tensor_tensor(out=tmpf[ti][:], in0=xT[:, kc, tsl(tt)], in1=rstd[:, tsl(tt)], op=ALU.mult),
                          reads=[xk(kc, tt), ("rstd", tt), "zc"], writes=[("tmpf", ti)])
                    if not final:
                        shoff = 0 if n == 0 else 48
                        P.add("act", lambda e, ti=ti, kc=kc, tt=tt, shoff=shoff: e.activation(
                            out=hT[:, kc, tsl(tt)], in_=tmpf[ti][:], func=AF.Identity,
                            bias=modT[:, l, shoff + kc, grp:grp + 1], scale=Atab[:, l, n, kc, grp:grp + 1]),
                            reads=[("tmpf", ti), "modT", "Atab"], writes=[hk(kc, tt)])
                    else:
                        so = (kc * TT + tt) % 16
                        ostg = hflat[:, so * 512:(so + 1) * 512]
                        okeys = [("hT", so, 0), ("hT", so, 1)]
                        P.add("act", lambda e, ti=ti, ostg=ostg, kc=kc: e.activation(
                            out=ostg, in_=tmpf[ti][:], func=AF.Identity, scale=gfinT[:, kc:kc + 1]),
                            reads=[("tmpf", ti), "gfinT"] + okeys, writes=okeys)
                        P.dma("sp", y_d.rearrange("(k p) t -> p k t", p=128)[:, kc, tsl(tt)], ostg,
                              reads=okeys, is_output=True)

        def big_mm(wv, nk, rhs_fn, rhs_keys_fn, s, pb):
            for tt in tts:
                def mm(e, tt=tt):
                    ins = None
                    for k in range(nk):
                        ins = e.matmul(bank(pb + tt), lhsT=wv[:, k, :], rhs=rhs_fn(k, tt), start=(k == 0), stop=(k == nk - 1))
                    return ins
                P.add("pe", mm, reads=[("ws", s)] + [rhs_keys_fn(k, tt) for k in range(nk)], writes=[("ps", pb + tt)])

        pbc = [0]

        def next_pb():
            pb = (pbc[0] % 2) * 2
            pbc[0] += 1
            return pb

        def h_rhs(k, tt):
            return hT[:, k, tsl(tt)]

        NCt = NC
        L8 = 8
        NBs = NCs // L8
        NB = NCt // L8
        CS = NBs + 1
        GB = 2048 // NC
        MB = GB // 8
        NBATCH = 64 // GB
        xs_words = GB * NCt * 2
        XSflat = mix[:, 8:16, :].rearrange("p a b -> p (a b)").bitcast(F32)[:, 0:xs_words]
        XS = XSflat.rearrange("p (g c r) -> p g c r", g=GB, r=2)
        XB = XSflat.rearrange("p (g b j r) -> p g b j r", g=GB, j=L8, r=2)
        xs_keys = ["XS"] + [("mix", j, tt) for j in range(8, 16) for tt in tts]
        nB2 = GB * NB * 2
        assert nB2 <= 512
        t1 = gcs[:, 0:nB2].rearrange("p (g b r) -> p g b r", g=GB, r=2)
        t2 = gcs[:, 512:512 + nB2].rearrange("p (g b r) -> p g b r", g=GB, r=2)
        tmpA = zc[:, 0:nB2].rearrange("p (g b r) -> p g b r", g=GB, r=2)
        tmpB = zc[:, 512:512 + nB2].rearrange("p (g b r) -> p g b r", g=GB, r=2)
        c1 = gcs[:, 0:GB * nseq * 2].rearrange("p (g s r) -> p g s r", g=GB, r=2)
        c2 = gcs[:, 512:512 + GB * nseq * 2].rearrange("p (g s r) -> p g s r", g=GB, r=2)
        assert GB * nseq * CS * 2 <= 640
        carr = carry[:, 0:GB * nseq * CS * 2].rearrange("p (g c r) -> p g c r", g=GB, r=2)
        tkeys = ["gcs", "zc", "acc", ("rstd", 0), ("rstd", 1)]

        def chk(n, l):
            if stop == ("ssm%d" % n, grp, l):
                raise StopBuild()

        def ush_v(ml, buf):
            return Ush[:, buf].rearrange("p g c -> p (g c)")[:, ml * 8 * NC:(ml + 1) * 8 * NC].rearrange("p (g c) -> p g c", c=NC)

        Ysb_v = Ysb[:].rearrange("p g c -> p (g c)")[:, 0:8 * NC].rearrange("p (g c) -> p g c", c=NC)

        def ssm_preA(l, mu, ml, buf):
            psU = psum[:, 4 * 512:6 * 512]

            def shuf(e):
                ins = None
                for glo in range(4):
                    for tau in range(8):
                        for rb in range(2):
                            ins = e.matmul(psU[:, rb * 512 + glo * NC:rb * 512 + (glo + 1) * NC], lhsT=selA[64 * rb:64 * rb + 64, glo, tau, :],
                                           rhs=u_sb[64 * rb:64 * rb + 64, tau:NT:8], start=(tau == 0), stop=(tau == 7))
                return ins
            P.add("pe", shuf, reads=["u_sb", "selA"], writes=[("ps", 4), ("ps", 5)])
            ushf = Ush[:, buf].rearrange("p g c -> p (g c)")
            for hb in range(2):
                P.add("act", lambda e, hb=hb: e.copy(out=ushf[:, ml * 8 * NC + hb * 4 * NC:ml * 8 * NC + (hb + 1) * 4 * NC], in_=psU[:, hb * 512:hb * 512 + 4 * NC]),
                      reads=[("ps", 4 + hb)], writes=[("Ush", buf, ml)])

        def load_B(l, mu):
            P.dma("sp", ssmwB[:], ssmw_d[l, mu].rearrange("p (g s c) -> p g s c", g=8, s=5)[:, :, 0:2, :],
                  reads=[("ssmw_d", l, mu)], writes=["ssmwB"])

        def load_C(l, mu):
            P.dma("sp", ssmwC[:], ssmw_d[l, mu].rearrange("p (g s c) -> p g s c", g=8, s=5)[:, :, 2:5, :],
                  reads=[("ssmw_d", l, mu)], writes=["ssmwC"])

        def ssm_preB(l, mu, ml, buf):
            if ml > 0:
                load_B(l, mu)
            uv = ush_v(ml, buf)
            for rnd in range(2):
                psS = psum[:, 6 * 512:6 * 512 + 4 * 2 * NC]

                def bmm(e, rnd=rnd, psS=psS):
                    ins = None
                    for gi in range(4):
                        g8 = rnd * 4 + gi
                        for r in range(2):
                            ins = e.matmul(psS[:, (gi * 2 + r) * NC:(gi * 2 + r + 1) * NC], lhsT=ssmwB[:, g8, r, :], rhs=uv[:, g8, :],
                                           start=True, stop=True)
                    return ins
                P.add("pe", bmm, reads=["ssmwB", ("Ush", buf, ml)], writes=[("ps", 6), ("ps", 7)])
                psS4 = psS.rearrange("p (g r c) -> p g r c", g=4, r=2)
                g0 = ml * 8 + rnd * 4
                for sq_i in range(nseq):
                    c0 = sq_i * NCs
                    P.add("act", lambda e, g0=g0, c0=c0, psS4=psS4: e.copy(
                        out=XS[0:64, g0:g0 + 4, c0:c0 + NCs, :].rearrange("p g c r -> p g r c"),
                        in_=psS4[0:64, :, :, c0:c0 + NCs]), reads=[("ps", 6), ("ps", 7), "XSclaim"], writes=[("XSe", ml, rnd, sq_i, 0)])
                    P.add("dve", lambda e, g0=g0, c0=c0, psS4=psS4: e.tensor_copy(
                        out=XS[64:128, g0:g0 + 4, c0:c0 + NCs, :].rearrange("p g c r -> p g r c"),
                        in_=psS4[64:128, :, :, c0 + NCs - 1::-1][:, :, :, 0:NCs] if c0 > 0 else psS4[64:128, :, :, NCs - 1::-1]),
                        reads=[("ps", 6), ("ps", 7), "XSclaim"], writes=[("XSe", ml, rnd, sq_i, 1)])

        def cmul(dst, src, src_sw, A1, A2, ta_, tb_, rk, wk):
            tt_op("dve", ta_, src, A1, ALU.mult, rk, tkeys)
            tt_op("dve", tb_, src_sw, A2, ALU.mult, rk + tkeys, tkeys)
            tt_op("dve", dst, ta_, tb_, ALU.add, tkeys + wk, wk)

        evac_keys = [("XSe", ml_, rnd_, sq_, hf_) for ml_ in range(MB) for rnd_ in range(2) for sq_ in range(nseq) for hf_ in range(2)]

        def scan(l, pair):
            P.add("dve", lambda e: e.memset(dummy[:], 0.0), reads=evac_keys + xs_keys + ["XSclaim"], writes=xs_keys)
            gs16 = slice(pair * GB, pair * GB + GB)
            A1b = a8tab[:, l, gs16, 0:2].unsqueeze(2).to_broadcast([128, GB, NB, 2])
            A2b = a8tab[:, l, gs16, 2:4].unsqueeze(2).to_broadcast([128, GB, NB, 2])
            B1 = a64tab[:, l, gs16, 0:2].unsqueeze(2).to_broadcast([128, GB, nseq, 2])
            B2 = a64tab[:, l, gs16, 2:4].unsqueeze(2).to_broadcast([128, GB, nseq, 2])
            for j in range(1, L8):
                cmul(t1, XB[:, :, :, j - 1, :], XB[:, :, :, j - 1, ::-1], A1b, A2b, t1, t2, xs_keys + ["a8tab"], tkeys)
                tt_op("dve", XB[:, :, :, j, :], XB[:, :, :, j, :], t1, ALU.add, xs_keys + tkeys, xs_keys)
            if grp == 0:
                P.add("dve", lambda e: e.memset(carr[:, :, 0:(nseq - 1) * CS + 1:CS, :], 0.0), reads=["carry"], writes=["carry"])
            else:
                P.add("dve", lambda e: e.tensor_copy(out=carr[:, :, 0, :], in_=sinit[:, l, gs16, :]), reads=["sinit", "carry"], writes=["carry"])
            for b in range(NBs):
                cprev = carr[:, :, b:b + (nseq - 1) * CS + 1:CS, :]
                cprev_sw = carr[:, :, b:b + (nseq - 1) * CS + 1:CS, ::-1]
                cnext = carr[:, :, b + 1:b + 1 + (nseq - 1) * CS + 1:CS, :]
                xfin = XB[:, :, b:b + (nseq - 1) * NBs + 1:NBs, L8 - 1, :]
                cmul(c1, cprev, cprev_sw, B1, B2, c1, c2, ["carry", "a64tab"], tkeys)
                tt_op("dve", cnext, c1, xfin, ALU.add, tkeys + xs_keys + ["carry"], ["carry"])
            cin = carr[:, :, 0:NB, :]
            cur = None
            for j in range(L8):
                dstt = tmpA if j % 2 == 0 else tmpB
                if j == 0:
                    if nseq == 1:
                        src, src_sw = cin, carr[:, :, 0:NB, ::-1]
                        cmul(dstt, src, src_sw, A1b, A2b, t1, t2, ["carry", "a8tab"], tkeys)
                    else:
                        for sq_i in range(nseq):
                            bs = slice(sq_i * NBs, (sq_i + 1) * NBs)
                            a1s = a8tab[:, l, gs16, 0:2].unsqueeze(2).to_broadcast([128, GB, NBs, 2])
                            a2s = a8tab[:, l, gs16, 2:4].unsqueeze(2).to_broadcast([128, GB, NBs, 2])
                            cmul(dstt[:, :, bs, :], carr[:, :, sq_i * CS:sq_i * CS + NBs, :], carr[:, :, sq_i * CS:sq_i * CS + NBs, ::-1],
                                 a1s, a2s, t1[:, :, bs, :], t2[:, :, bs, :], ["carry", "a8tab"], tkeys)
                else:
                    cmul(dstt, cur, cur[:, :, :, ::-1], A1b, A2b, t1, t2, ["a8tab"] + tkeys, tkeys)
                cur = dstt
                tt_op("dve", XB[:, :, :, j, :], XB[:, :, :, j, :], cur, ALU.add, xs_keys + tkeys, xs_keys)
            if grp == 0:
                P.add("act", lambda e: e.copy(out=nst_sb[:, :, l, gs16, :].rearrange("p s g r -> p g s r"),
                                              in_=carr[:, :, NBs:NBs + (nseq - 1) * CS + 1:CS, :]),
                      reads=["carry"], writes=["nst_sb"])

        NHB = (8 * NC + 511) // 512
        YSK = [("Ysb", hb) for hb in range(NHB)]

        def evac_y_top():
            psY_ = psum[:, 4 * 512:6 * 512]
            ysf_ = Ysb[:].rearrange("p g c -> p (g c)")
            for hb in range(NHB):
                P.add("act", lambda e, hb=hb: e.copy(out=ysf_[:, hb * 512:(hb + 1) * 512], in_=psY_[:, hb * 512:(hb + 1) * 512]),
                      reads=[("ps", 4 + hb)], writes=[("Ysb", hb)])

        def ssm_post(l, mu, ml, buf, phase="all"):
            psY = psum[:, 4 * 512:6 * 512]
            ysf = Ysb[:].rearrange("p g c -> p (g c)")
            nhb = (8 * NC + 511) // 512
            ysk = [("Ysb", hb) for hb in range(nhb)]

            def evac_y():
                for hb in range(nhb):
                    P.add("act", lambda e, hb=hb: e.copy(out=ysf[:, hb * 512:(hb + 1) * 512], in_=psY[:, hb * 512:(hb + 1) * 512]),
                          reads=[("ps", 4 + hb)], writes=[("Ysb", hb)])

            if phase in ("all", "a"):
                post_a(l, mu, ml, buf)
                if phase == "all" or ml == 0:
                    evac_y()
            if phase in ("all", "b"):
                if phase == "b" and ml > 0:
                    evac_y()
                post_b(l, mu, ysk)

        def post_a(l, mu, ml, buf):
            if ml > 0:
                load_C(l, mu)
            gsl8 = slice(ml * 8, ml * 8 + 8)
            for sq_i in range(nseq):
                c0 = sq_i * NCs
                cb = sq_i * CS
                P.add("act", lambda e, c0=c0: e.copy(out=Xbf[0:64, :, :, c0 + 1:c0 + NCs],
                                                     in_=XS[0:64, gsl8, c0:c0 + NCs - 1, :].rearrange("p g c r -> p g r c")),
                      reads=xs_keys + ["Xbf"], writes=["Xbf"])
                P.add("act", lambda e, c0=c0, cb=cb: e.copy(out=Xbf[0:64, :, :, c0], in_=carr[0:64, gsl8, cb, :]),
                      reads=["carry", "Xbf"], writes=["Xbf"])
                P.add("dve", lambda e, c0=c0: e.tensor_copy(
                    out=Xbf[64:128, :, :, c0:c0 + NCs - 1],
                    in_=(XS[64:128, gsl8, c0 + NCs - 2::-1, :][:, :, 0:NCs - 1, :] if c0 > 0 else XS[64:128, gsl8, NCs - 2::-1, :]).rearrange("p g c r -> p g r c")),
                    reads=xs_keys + ["XbfB"], writes=["XbfB"])
                P.add("dve", lambda e, c0=c0, cb=cb: e.tensor_copy(out=Xbf[64:128, :, :, c0 + NCs - 1], in_=carr[64:128, gsl8, cb, :]),
                      reads=["carry", "XbfB"], writes=["XbfB"])
            uv = ush_v(ml, buf)
            psY = psum[:, 4 * 512:6 * 512]

            def ymm(e):
                ins = None
                for g8 in range(8):
                    o = psY[:, g8 * NC:(g8 + 1) * NC]
                    e.matmul(o, lhsT=ssmwC[:, g8, 0, :], rhs=uv[:, g8, :], start=True, stop=False)
                    for r in range(2):
                        ins = e.matmul(o, lhsT=ssmwC[:, g8, 1 + r, :], rhs=Xbf[:, g8, r, 0:NC], start=False, stop=(r == 1))
                return ins
            P.add("pe", ymm, reads=["ssmwC", "Xbf", "XbfB", ("Ush", buf, ml)], writes=[("ps", 4), ("ps", 5)])

        def post_b(l, mu, ysk):
            for tt in tts:
                def unsh(e, tt=tt):
                    ins = None
                    for tlo in range(4):
                        for g8 in range(8):
                            for tb_ in range(2):
                                t = tb_ * 4 + tlo
                                ob = bank(0 + tt) if tb_ == 0 else bank(2 + tt)
                                ins = e.matmul(ob[:, t:512:8], lhsT=selA[64 * tb_:64 * tb_ + 64, tlo, g8, :],
                                               rhs=Ysb_v[64 * tb_:64 * tb_ + 64, g8, tt * 64:(tt + 1) * 64], start=(g8 == 0), stop=(g8 == 7))
                    return ins
                P.add("pe", unsh, reads=ysk + ["selA"], writes=[("ps", 0 + tt), ("ps", 2 + tt)])
                for hf in range(2):
                    bk = (0 + tt) if hf == 0 else (2 + tt)
                    src = bank(bk).rearrange("p (c t) -> p c t", t=8)[:, :, 4 * hf:4 * hf + 4]
                    dst = mix[:, mu, tsl(tt)].rearrange("p (c t) -> p c t", t=8)[:, :, 4 * hf:4 * hf + 4]
                    P.add("act", lambda e, src=src, dst=dst: e.activation(out=dst, in_=src, func=AF.Gelu_apprx_tanh),
                          reads=[("ps", bk)], writes=[mk(mu, tt)])

        for l in range(2):
            norm(l, 0)
            if stop == ("norm1", grp, l):
                return True
            win = w_in[l].rearrange("(k p) c -> p k c", p=128)
            def pre_A(pair):
                for ml in range(MB):
                    mu = pair * MB + ml
                    s, wv = wload(win[:, :, mu * 128:(mu + 1) * 128], 16)
                    pb = next_pb()
                    big_mm(wv, 16, h_rhs, hk, s, pb)
                    for tt in tts:
                        P.add("act", lambda e, tt=tt, pb=pb: e.copy(out=u_sb[:, tsl(tt)], in_=bank(pb + tt)), reads=[("ps", pb + tt), "u_sb"], writes=["u_sb"])
                    ssm_preA(l, mu, ml, pair % 2)

            def pre_B(pair):
                P.add("dve", lambda e: e.memset(dummy[:], 0.0), reads=xs_keys, writes=xs_keys + ["XSclaim"])
                for ml in range(MB):
                    ssm_preB(l, pair * MB + ml, ml, pair % 2)

            pre_A(0)
            load_B(l, 0)
            pre_B(0)
            for pair in range(NBATCH):
                if pair + 1 < NBATCH:
                    pre_A(pair + 1)
                    load_B(l, (pair + 1) * MB)
                load_C(l, pair * MB)
                chk(2, l)
                scan(l, pair)
                chk(3, l)
                if MB == 2:
                    for ml in range(MB):
                        ssm_post(l, pair * MB + ml, ml, pair % 2, phase="a")
                    if pair + 1 < NBATCH:
                        pre_B(pair + 1)
                    for ml in range(MB):
                        ssm_post(l, pair * MB + ml, ml, pair % 2, phase="b")
                else:
                    buf = pair % 2
                    m0 = pair * MB
                    post_a(l, m0 + 0, 0, buf)
                    evac_y_top()
                    post_a(l, m0 + 1, 1, buf)
                    post_b(l, m0 + 0, YSK)
                    evac_y_top()
                    post_a(l, m0 + 2, 2, buf)
                    post_b(l, m0 + 1, YSK)
                    evac_y_top()
                    post_a(l, m0 + 3, 3, buf)
                    if pair + 1 < NBATCH:
                        pre_B(pair + 1)
                    post_b(l, m0 + 2, YSK)
                    evac_y_top()
                    post_b(l, m0 + 3, YSK)
                chk(5, l)
            if stop == ("ssmall", grp, l):
                return True
            for j in range(8):
                s, wv = wload(win[:, :, 2048 + j * 128:2048 + (j + 1) * 128], 16)
                pb = next_pb()
                big_mm(wv, 16, h_rhs, hk, s, pb)
                for tt in tts:
                    P.add("act", lambda e, tt=tt, pb=pb: e.copy(out=gcs[:, tsl(tt)], in_=bank(pb + tt)), reads=[("ps", pb + tt), "gcs"], writes=["gcs"])
                s, wv = wload(win[:, :, 3072 + j * 128:3072 + (j + 1) * 128], 16)
                pb = next_pb()
                big_mm(wv, 16, h_rhs, hk, s, pb)
                for tt in tts:
                    P.add("dve", lambda e, tt=tt, pb=pb: e.tensor_tensor(out=zc[:, tsl(tt)], in0=gcs[:, tsl(tt)], in1=bank(pb + tt), op=ALU.mult),
                          reads=[("ps", pb + tt), "gcs", "zc", ("rstd", 0), ("rstd", 1)], writes=["zc", ("rstd", 0), ("rstd", 1)])
                P.add("act", lambda e, j=j, l=l: e.activation(out=acc[:, 0:NT], in_=zc[:, 0:NT], func=AF.Identity,
                                                         bias=convb[:, l, j:j + 1], scale=convw[:, l, 1, j:j + 1]),
                      reads=["zc", "convw", "convb", "acc", "gcs"], writes=["acc", "gcs"])
                RL = 64 if grp == 1 else 256
                acc3 = acc[:, 0:NT].rearrange("p (r c) -> p r c", c=RL)
                zc3 = zc[:, 0:NT].rearrange("p (r c) -> p r c", c=RL)
                P.add("dve", lambda e, j=j, acc3=acc3, zc3=zc3, RL=RL, l=l: e.scalar_tensor_tensor(
                    out=acc3[:, :, 1:RL], in0=zc3[:, :, 0:RL - 1], scalar=convw[:, l, 0, j:j + 1], in1=acc3[:, :, 1:RL], op0=ALU.mult, op1=ALU.add),
                    reads=["zc", "convw", "acc", "gcs"], writes=["acc", "gcs"])
                P.add("dve", lambda e, j=j, acc3=acc3, zc3=zc3, RL=RL, l=l: e.scalar_tensor_tensor(
                    out=acc3[:, :, 0:RL - 1], in0=zc3[:, :, 1:RL], scalar=convw[:, l, 2, j:j + 1], in1=acc3[:, :, 0:RL - 1], op0=ALU.mult, op1=ALU.add),
                    reads=["zc", "convw", "acc", "gcs"], writes=["acc", "gcs"])
                s, wv = wload(win[:, :, 1024 + j * 128:1024 + (j + 1) * 128], 16)
                pb = next_pb()
                big_mm(wv, 16, h_rhs, hk, s, pb)
                for tt in tts:
                    P.add("dve", lambda e, tt=tt, pb=pb, j=j: e.tensor_tensor(out=mix[:, 8 + j, tsl(tt)], in0=acc[:, tsl(tt)], in1=bank(pb + tt), op=ALU.mult),
                          reads=[("ps", pb + tt), "acc", "gcs"], writes=[mk(8 + j, tt)])
            if stop == ("mixer", grp, l):
                return True
            wgl = w_glu[l].rearrange("(k p) c -> p k c", p=128)
            for m in range(8):
                s, wv = wload(wgl[:, :, m * 128:(m + 1) * 128], 8)
                pb = next_pb()
                big_mm(wv, 8, lambda k, tt: mix[:, k, tsl(tt)], mk, s, pb)
                for tt in tts:
                    ti = counters["tmp"] % 2
                    counters["tmp"] += 1
                    P.add("act", lambda e, tt=tt, pb=pb, ti=ti: e.activation(out=tmpf[ti][:], in_=bank(pb + tt), func=AF.Sigmoid),
                          reads=[("ps", pb + tt)], writes=[("tmpf", ti)])
                    P.add("dve", lambda e, tt=tt, m=m, ti=ti: e.tensor_tensor(out=hT[:, m, tsl(tt)], in0=tmpf[ti][:], in1=mix[:, m, tsl(tt)], op=ALU.mult),
                          reads=[("tmpf", ti), mk(m, tt)], writes=[hk(m, tt)])
            if stop == ("glu", grp, l):
                return True
            wov = w_out[l].rearrange("(k p) c -> p k c", p=128)
            for dt_ in range(NKC):
                s, wv = wload(wov[:, :, dt_ * 128:(dt_ + 1) * 128], 16)
                pb = next_pb()
                big_mm(wv, 16, lambda k, tt: (hT[:, k, tsl(tt)] if k < 8 else mix[:, k, tsl(tt)]),
                       lambda k, tt: (hk(k, tt) if k < 8 else mk(k, tt)), s, pb)
                for tt in tts:
                    P.add("dve", lambda e, tt=tt, pb=pb, dt_=dt_, l=l: e.scalar_tensor_tensor(
                        out=xT[:, dt_, tsl(tt)], in0=bank(pb + tt), scalar=modT[:, l, 32 + dt_, grp:grp + 1], in1=xT[:, dt_, tsl(tt)],
                        op0=ALU.mult, op1=ALU.add), reads=[("ps", pb + tt), "modT", xk(dt_, tt)], writes=[xk(dt_, tt)])
            if stop == ("wout", grp, l):
                return True
            norm(l, 1)
            wgv = w_gate[l].rearrange("(k p) c -> p k c", p=128)
            wuv = w_up[l].rearrange("(k p) c -> p k c", p=128)
            wdv = w_down[l].rearrange("(j p) c -> p j c", p=128)
            for j0 in range(0, NFF, 16):
                nj = min(16, NFF - j0)
                for jj in range(nj):
                    j = j0 + jj
                    sg_, wg_ = wload(wgv[:, :, j * 128:(j + 1) * 128], 16)
                    su_, wu_ = wload(wuv[:, :, j * 128:(j + 1) * 128], 16)
                    big_mm(wg_, 16, h_rhs, hk, sg_, 0)
                    big_mm(wu_, 16, h_rhs, hk, su_, 2)
                    for tt in tts:
                        ti = counters["tmp"] % 2
                        counters["tmp"] += 1
                        P.add("act", lambda e, tt=tt, ti=ti: e.activation(out=tmpf[ti][:], in_=bank(0 + tt), func=AF.Silu),
                              reads=[("ps", 0 + tt)], writes=[("tmpf", ti)])
                        P.add("dve", lambda e, tt=tt, ti=ti, jj=jj: e.tensor_tensor(out=mix[:, jj, tsl(tt)], in0=tmpf[ti][:], in1=bank(2 + tt), op=ALU.mult),
                              reads=[("tmpf", ti), ("ps", 2 + tt)], writes=[mk(jj, tt)])
                for dt_ in range(NKC):
                    s, wv = wload(wdv[:, j0:j0 + nj, dt_ * 128:(dt_ + 1) * 128], nj)
                    pb = 4 + (dt_ % 2) * 2
                    big_mm(wv, nj, lambda k, tt: mix[:, k, tsl(tt)], mk, s, pb)
                    for tt in tts:
                        P.add("dve", lambda e, tt=tt, pb=pb, dt_=dt_, l=l: e.scalar_tensor_tensor(
                            out=xT[:, dt_, tsl(tt)], in0=bank(pb + tt), scalar=modT[:, l, 80 + dt_, grp:grp + 1], in1=xT[:, dt_, tsl(tt)],
                            op0=ALU.mult, op1=ALU.add), reads=[("ps", pb + tt), "modT", xk(dt_, tt)], writes=[xk(dt_, tt)])
            if stop == ("ffn", grp, l):
                return True
        norm(0, 0, final=True)
        if grp == 0:
            P.dma("sp", nst_d, nst_sb[:].rearrange("p s l g r -> p (s l g r)"), reads=["nst_sb"], is_output=True)

    try:
        wmode["mode"], wmode["tid"] = "save", 0
        if not run_pass(1, 1024, xS, yS, [0]):
            assert wmode["tid"] == NWT, wmode["tid"]
            wmode["mode"], wmode["tid"] = "reuse", 0
            run_pass(0, 512, xP, yP, [0, 1])
    except StopBuild:
        pass

    P.emit()
    for cm in reversed(cms):
        cm.__exit__(None, None, None)
    return nc


_NC_CACHE = {}


def _host_consts():
    ident = np.eye(128, dtype=np.float32)
    tau = np.arange(128) // 16
    mf = (tau[:, None] <= tau[None, :]).astype(np.float32)
    mb = (tau[:, None] >= tau[None, :]).astype(np.float32)
    cf = np.concatenate([ident, mf, mb], axis=1)
    sel = np.zeros((128, 4, 8, 128), np.float32)
    for rb in range(2):
        for a in range(4):
            for h in range(16):
                for t in range(8):
                    sel[64 * rb + a * 16 + h, a, t, t * 16 + h] = 1.0
    return cf, sel.reshape(128, 4096)


def kernel(x_prompt, x_sample, state_ssm, c, c_ctx, w_ada, b_ada, g_mix, w_in,
           ssm_lam_re, ssm_lam_im, ssm_log_dt, ssm_b_re, ssm_b_im, ssm_c_re, ssm_c_im,
           ssm_d, w_glu, conv_w, conv_b, w_out, g_ffn, w_gate, w_up, w_down, g_final):
    f = lambda a: np.ascontiguousarray(np.asarray(a, dtype=np.float32))
    x_prompt, x_sample, state_ssm, c, c_ctx = map(f, (x_prompt, x_sample, state_ssm, c, c_ctx))
    if "nc" not in _NC_CACHE:
        _NC_CACHE["nc"] = build_nc()
    nc = _NC_CACHE["nc"]
    cf, sel = _host_consts()
    shared = {
        "w_ada": f(w_ada), "w_in": f(w_in), "w_glu": f(w_glu), "w_out": f(w_out),
        "w_gate": f(w_gate), "w_up": f(w_up), "w_down": f(w_down),
        "b_adaT": f(np.asarray(b_ada).reshape(2, 96, 128).transpose(2, 0, 1).reshape(128, 192)),
        "gmixT": f(np.asarray(g_mix).reshape(2, 16, 128).transpose(2, 0, 1).reshape(128, 32)),
        "gffnT": f(np.asarray(g_ffn).reshape(2, 16, 128).transpose(2, 0, 1).reshape(128, 32)),
        "gfinT": f(np.asarray(g_final).reshape(16, 128).T),
        "convw": f(np.asarray(conv_w).reshape(2, 3, 8, 128).transpose(3, 0, 1, 2).reshape(128, 48)),
        "convb": f(np.asarray(conv_b).reshape(2, 8, 128).transpose(2, 0, 1).reshape(128, 16)),
        "dtab": f(np.broadcast_to(np.asarray(ssm_d).reshape(2, 64, 16).transpose(2, 0, 1)[None], (8, 16, 2, 64)).reshape(128, 128)),
        "lamre": f(np.asarray(ssm_lam_re).transpose(1, 3, 0, 2).reshape(128, 128)),
        "lamim": f(np.asarray(ssm_lam_im).transpose(1, 3, 0, 2).reshape(128, 128)),
        "logdt": f(np.broadcast_to(np.asarray(ssm_log_dt).transpose(1, 0, 2)[:, None], (2, 64, 2, 64)).reshape(128, 128)),
        "Bre_in": f(np.asarray(ssm_b_re).transpose(1, 3, 0, 2, 4).reshape(128, 2048)),
        "Bim_in": f(np.asarray(ssm_b_im).transpose(1, 3, 0, 2, 4).reshape(128, 2048)),
        "Cre_in": f(np.asarray(ssm_c_re).transpose(1, 4, 0, 2, 3).reshape(128, 2048)),
        "Cim_in": f(np.asarray(ssm_c_im).transpose(1, 4, 0, 2, 3).reshape(128, 2048)),
        "cF32": cf, "selA_f": sel,
    }
    in_maps = []
    for i in range(8):
        m = dict(shared)
        m["xP"] = f(x_prompt[2 * i:2 * i + 2].reshape(512, D).T)
        m["xS"] = f(x_sample[i].T)
        cond = np.stack([c_ctx.reshape(16, 128).T, c[i].reshape(16, 128).T], axis=2)
        m["condT"] = f(cond.reshape(128, 32))
        m["sinit"] = f(state_ssm[i].transpose(1, 4, 0, 3, 2).reshape(128, 256))
        in_maps.append(m)
    res = run_bass_kernel_spmd(nc, in_maps, core_ids=list(range(8)))
    y_prompt = np.empty((16, 256, D), np.float32)
    y_sample = np.empty((8, 1024, D), np.float32)
    new_state = np.empty((16, 2, 2, 2, 64, 64), np.float32)
    for i in range(8):
        r = res.results[i]
        y_prompt[2 * i:2 * i + 2] = np.asarray(r["yP"]).T.reshape(2, 256, D)
        y_sample[i] = np.asarray(r["yS"]).T
        ns = np.asarray(r["nst"]).reshape(2, 64, 2, 2, 64, 2)
        new_state[2 * i:2 * i + 2] = ns.transpose(2, 3, 0, 5, 4, 1)
    return (y_prompt, y_sample, new_state)
```

```python
import numpy as np
import ml_dtypes
import concourse.bass as bass
import concourse.mybir as mybir
from concourse.bass_utils import run_bass_kernel_spmd

F32 = mybir.dt.float32
BF16 = mybir.dt.bfloat16
I32 = mybir.dt.int32
AF = mybir.ActivationFunctionType
ALU = mybir.AluOpType

D = 2048
NKC = 16
DFF = 5632
NFF = 44
DIN = 4096
NG = 64
PI = float(np.pi)
NST = 4


class StopBuild(Exception):
    pass


class Op:
    __slots__ = ("eng", "fn", "deps", "signals", "sigval", "is_dma", "dsem", "dval", "prev_dma", "name")

    def __init__(self, eng, fn, is_dma, name):
        self.eng = eng
        self.fn = fn
        self.deps = set()
        self.signals = False
        self.sigval = 0
        self.is_dma = is_dma
        self.dsem = None
        self.dval = 0
        self.prev_dma = None
        self.name = name


class Prog:
    ENG = ("pe", "act", "dve", "pool", "sp")

    def __init__(self, nc):
        self.nc = nc
        self.eng_ops = {e: [] for e in self.ENG}
        self.res_w = {}
        self.res_r = {}
        self.ctx = []
        self.engsem = {}
        for e in self.ENG:
            cm = nc.semaphore("s_" + e)
            self.engsem[e] = cm.__enter__()
            self.ctx.append(cm)
        n_dma_sems = {"sp": 8, "pool": 8}
        self.dsems = {}
        self.dcnt = {}
        self.dlast = {}
        for q, n in n_dma_sems.items():
            lst = []
            for i in range(n):
                cm = nc.semaphore("d_%s%d" % (q, i))
                lst.append(cm.__enter__())
                self.ctx.append(cm)
            self.dsems[q] = lst
            self.dcnt[q] = 0
            self.dlast[q] = [None] * n
        self.dvals = {}
        self.out_dmas = []

    def add(self, eng, fn, reads=(), writes=(), dma=False, name=""):
        op = Op(eng, fn, dma, name)
        deps = op.deps
        for r in reads:
            w = self.res_w.get(r)
            if w is not None:
                deps.add(w)
        for k in writes:
            w = self.res_w.get(k)
            if w is not None:
                deps.add(w)
            for rd in self.res_r.get(k, ()):
                deps.add(rd)
        for d in deps:
            d.signals = True
        for r in reads:
            lst = self.res_r.setdefault(r, [])
            if not dma:
                lst[:] = [o for o in lst if o.is_dma or o.eng != eng]
            lst.append(op)
        for k in writes:
            self.res_w[k] = op
            self.res_r[k] = []
        if dma:
            q = eng
            i = self.dcnt[q] % len(self.dsems[q])
            self.dcnt[q] += 1
            op.dsem = self.dsems[q][i]
            key = (q, i)
            self.dvals[key] = self.dvals.get(key, 0) + 16
            op.dval = self.dvals[key]
            op.prev_dma = self.dlast[q][i]
            self.dlast[q][i] = op
        self.eng_ops[eng].append(op)
        return op

    def dma(self, q, out, in_, reads=(), writes=(), name="", is_output=False):
        op = self.add(q, lambda e: e.dma_start(out=out, in_=in_), reads, writes, dma=True, name=name)
        if is_output:
            self.out_dmas.append(op)
        return op

    def emit(self):
        fin = Op("sp", lambda e: None, False, "final")
        fin.deps = set(self.out_dmas)
        self.eng_ops["sp"].append(fin)
        for e in self.ENG:
            c = 0
            for op in self.eng_ops[e]:
                if (not op.is_dma) and op.signals:
                    c += 1
                    op.sigval = c
        nc = self.nc
        with nc.Block() as block:
            @block.tensor
            def _(eng):
                self._emit_engine("pe", eng)

            @block.scalar
            def _(eng):
                self._emit_engine("act", eng)

            @block.vector
            def _(eng):
                self._emit_engine("dve", eng)

            @block.gpsimd
            def _(eng):
                self._emit_engine("pool", eng)

            @block.sync
            def _(eng):
                self._emit_engine("sp", eng)
        for cm in reversed(self.ctx):
            cm.__exit__(None, None, None)

    def _emit_engine(self, e, eng):
        waited = {}
        for op in self.eng_ops[e]:
            needs = {}
            for d in op.deps:
                if d.is_dma:
                    s, v = d.dsem, d.dval
                else:
                    if d.eng == e and e == "pe":
                        continue
                    s, v = self.engsem[d.eng], d.sigval
                k = id(s)
                if k not in needs or needs[k][1] < v:
                    needs[k] = (s, v)
            if op.is_dma and op.prev_dma is not None:
                s, v = op.prev_dma.dsem, op.prev_dma.dval
                k = id(s)
                if k not in needs or needs[k][1] < v:
                    needs[k] = (s, v)
            for k, (s, v) in needs.items():
                if waited.get(k, 0) < v:
                    eng.wait_ge(s, v)
                    waited[k] = v
            ins = op.fn(eng)
            if op.is_dma:
                ins.then_inc(op.dsem, 16)
            elif op.signals:
                ins.then_inc(self.engsem[e], 1)


def build_nc(debug=False, stop=None):
    nc = bass.Bass("TRN2", target_bir_lowering=False)

    def din(name, shape, dt=F32):
        return nc.dram_tensor(name, list(shape), dt, kind="ExternalInput").ap()

    def dout(name, shape, dt=F32):
        return nc.dram_tensor(name, list(shape), dt, kind="ExternalOutput").ap()

    xP = din("xP", [D, 512])
    xS = din("xS", [D, 1024])
    condT_d = din("condT", [128, 32])
    sinit_d = din("sinit", [128, 256])
    w_ada = din("w_ada", [2, D, 6 * D])
    b_adaT_d = din("b_adaT", [128, 192])
    gmixT_d = din("gmixT", [128, 32])
    gffnT_d = din("gffnT", [128, 32])
    gfinT_d = din("gfinT", [128, 16])
    w_in = din("w_in", [2, D, DIN])
    w_glu = din("w_glu", [2, 1024, 1024])
    w_out = din("w_out", [2, D, D])
    w_gate = din("w_gate", [2, D, DFF])
    w_up = din("w_up", [2, D, DFF])
    w_down = din("w_down", [2, DFF, D])
    convw_d = din("convw", [128, 48])
    convb_d = din("convb", [128, 16])
    dtab_d = din("dtab", [128, 128])
    lamre_d = din("lamre", [128, 128])
    lamim_d = din("lamim", [128, 128])
    logdt_d = din("logdt", [128, 128])
    Bre_d = din("Bre_in", [128, 2048])
    Bim_d = din("Bim_in", [128, 2048])
    Cre_d = din("Cre_in", [128, 2048])
    Cim_d = din("Cim_in", [128, 2048])
    cf32_d = din("cF32", [128, 384])
    selA_d = din("selA_f", [128, 4096])
    yP = dout("yP", [D, 512])
    yS = dout("yS", [D, 1024])
    nst_d = dout("nst", [128, 512])
    ssmw_d = nc.dram_tensor("ssmw_scratch", [2, 8, 128, 5120], BF16, kind="Internal").ap()
    NWT = 384
    wbf_d = nc.dram_tensor("wbf_scratch", [NWT, 128, 2048], BF16, kind="Internal").ap()

    P = Prog(nc)
    cms = []

    def sb(name, shape, dt):
        cm = nc.sbuf_tensor(name, list(shape), dt)
        t = cm.__enter__()
        cms.append(cm)
        return t

    xT = sb("xT", [128, NKC, 1024], F32)
    hT = sb("hT", [128, NKC, 1024], BF16)
    mix = sb("mix", [128, NKC, 1024], BF16)
    wst = [sb("wst%d" % i, [128, 2048], BF16) for i in range(NST)]
    identf = sb("identf", [128, 128], F32)
    maskf = sb("maskf", [128, 128], F32)
    maskb = sb("maskb", [128, 128], F32)
    selA = sb("selA", [128, 4, 8, 128], BF16)
    ones_bf = sb("ones_bf", [128, 128], BF16)
    epsc = sb("epsc", [128, 1], F32)
    condT = sb("condT_sb", [128, 16, 2], F32)
    scond = sb("scond", [128, 16, 2], BF16)
    modT = sb("modT", [128, 2, 96, 2], F32)
    Atab = sb("Atab", [128, 2, 2, 16, 2], F32)
    b_adaT = sb("b_adaT_sb", [128, 2, 96], F32)
    gmixT = sb("gmixT_sb", [128, 2, 16], F32)
    gffnT = sb("gffnT_sb", [128, 2, 16], F32)
    gfinT = sb("gfinT_sb", [128, 16], F32)
    convw = sb("convw_sb", [128, 2, 3, 8], F32)
    convb = sb("convb_sb", [128, 2, 8], F32)
    sinit = sb("sinit_sb", [128, 2, 64, 2], F32)
    a8tab = sb("a8tab", [128, 2, 64, 4], F32)
    a64tab = sb("a64tab", [128, 2, 64, 4], F32)
    nst_sb = sb("nst_sb", [128, 2, 2, 64, 2], F32)
    u_sb = sb("u_sb", [128, 1024], BF16)
    Ush = sb("Ush", [128, 2, 16, 128], BF16)
    carry = sb("carry", [128, 640], F32)
    Xbf = sb("Xbf", [128, 8, 2, 128], BF16)
    Ysb = sb("Ysb", [128, 8, 128], BF16)
    ssmwB = sb("ssmwB", [128, 8, 2, 128], BF16)
    ssmwC = sb("ssmwC", [128, 8, 3, 128], BF16)
    gcs = sb("gcs", [128, 1024], F32)
    zc = sb("zc", [128, 1024], F32)
    acc = gcs
    sq = [sb("sq%d" % i, [128, 512], BF16) for i in range(2)]
    rstd = zc
    tmpf = [sb("tmpf%d" % i, [128, 512], F32) for i in range(2)]

    pcm = nc.psum_tensor("psum_all", [128, 4096], F32)
    psum = pcm.__enter__()
    cms.append(pcm)

    def bank(b):
        return psum[:, b * 512:(b + 1) * 512]

    counters = {"ws": 0, "sq": 0, "tmp": 0}

    P.dma("sp", identf[:], cf32_d[:, 0:128], writes=["identf"])
    P.dma("sp", maskf[:], cf32_d[:, 128:256], writes=["maskf"])
    P.dma("sp", maskb[:], cf32_d[:, 256:384], writes=["maskb"])
    P.dma("pool", selA[:].rearrange("p a b c -> p (a b c)"), selA_d, writes=["selA"])
    P.add("dve", lambda e: e.memset(ones_bf[:], 1.0), writes=["ones_bf"])
    P.add("dve", lambda e: e.memset(epsc[:], 1e-6), writes=["epsc"])
    P.dma("sp", condT[:].rearrange("p a b -> p (a b)"), condT_d, writes=["condT"])
    P.dma("sp", b_adaT[:].rearrange("p a b -> p (a b)"), b_adaT_d, writes=["b_adaT"])
    P.dma("sp", gmixT[:].rearrange("p a b -> p (a b)"), gmixT_d, writes=["gmixT"])
    P.dma("sp", gffnT[:].rearrange("p a b -> p (a b)"), gffnT_d, writes=["gffnT"])
    P.dma("sp", gfinT[:], gfinT_d, writes=["gfinT"])
    P.dma("sp", convw[:].rearrange("p a b c -> p (a b c)"), convw_d, writes=["convw"])
    P.dma("sp", convb[:].rearrange("p a b -> p (a b)"), convb_d, writes=["convb"])
    P.dma("sp", sinit[:].rearrange("p a b c -> p (a b c)"), sinit_d, writes=["sinit"])
    P.add("act", lambda e: e.activation(out=scond[:], in_=condT[:], func=AF.Silu), reads=["condT"], writes=["scond"])

    xflat = xT[:].rearrange("p a b -> p (a b)")
    hflat = hT[:].rearrange("p a b -> p (a b)").bitcast(F32)

    class Carve:
        def __init__(self, flat):
            self.flat = flat
            self.off = 0

        def get(self, n):
            v = self.flat[:, self.off:self.off + n]
            self.off += n
            return v

    cx = Carve(xflat)
    lamre = cx.get(128)
    lamim = cx.get(128)
    logdt = cx.get(128)
    dtab = cx.get(128)
    Bre_in = cx.get(1024)
    Bim_in = cx.get(1024)
    Cre_in = cx.get(1024)
    Cim_in = cx.get(1024)
    P.dma("sp", lamre, lamre_d, writes=["lamre"])
    P.dma("sp", lamim, lamim_d, writes=["lamim"])
    P.dma("sp", logdt, logdt_d, writes=["logdt"])
    P.dma("sp", dtab, dtab_d, writes=["dtab"])
    pro_keys = ["lamre", "lamim", "logdt", "dtab", "Bre_in", "Bim_in", "Cre_in", "Cim_in"]

    def tab(name, n):
        pro_keys.append(name)
        return cx.get(n)

    dtt = tab("dtt", 128)
    lr = tab("lr", 128)
    lrdt = tab("lrdt", 128)
    th = tab("th", 128)
    kf = tab("kf", 128)
    ki_f = tab("ki", 128)
    ki = ki_f.bitcast(I32)
    ang = tab("ang", 128)
    fix = tab("fix", 128)
    sin1 = tab("sin1", 128)
    cos1 = tab("cos1", 128)
    mag1 = tab("mag1", 128)
    crk = tab("crk", 9 * 128)
    cik = tab("cik", 9 * 128)
    mN = tab("mN", 128)
    crN = tab("crN", 8 * 128)
    ciN = tab("ciN", 8 * 128)
    ta = tab("ta", 128)
    tb = tab("tb", 128)
    den = tab("den", 128)
    wr = tab("wr", 128)
    wi = tab("wi", 128)
    crB = tab("crB", 512)
    ciB = tab("ciB", 512)
    crC = tab("crC", 512)
    ciC = tab("ciC", 512)
    crR = tab("crR", 512)
    ciR = tab("ciR", 512)
    assert cx.off <= 16384, cx.off
    ch = Carve(hflat)
    bbr = ch.get(128)
    bbi = ch.get(128)
    BTre = ch.get(1024)
    BTim = ch.get(1024)
    Rre = ch.get(1024)
    Rimn = ch.get(1024)
    CTre = ch.get(1024)
    CTimn = ch.get(1024)
    tq = gcs[:]
    tq2 = zc[:]
    assert ch.off <= 8192
    stg_flat = mix[:].rearrange("p a b -> p (a b)")
    stg = [stg_flat[:, i * 5120:(i + 1) * 5120] for i in range(2)]
    def dv(fn, reads, writes, eng="dve"):
        return P.add(eng, fn, reads=reads, writes=writes)

    def tt_op(eng, out, a, b, op, reads, writes):
        return P.add(eng, lambda e: e.tensor_tensor(out=out, in0=a, in1=b, op=op), reads=reads, writes=writes)

    P.add("act", lambda e: e.activation(out=dtt, in_=logdt, func=AF.Exp), reads=["logdt"], writes=["dtt"])
    dv(lambda e: e.tensor_single_scalar(out=lr, in_=lamre, scalar=-1e-4, op=ALU.min), ["lamre"], ["lr"])
    tt_op("dve", lrdt, lr, dtt, ALU.mult, ["lr", "dtt"], ["lrdt"])
    tt_op("dve", th, lamim, dtt, ALU.mult, ["lamim", "dtt"], ["th"])

    def range_reduce_sin(dst, shift, name):
        dv(lambda e: e.tensor_scalar(out=kf, in0=th, scalar1=shift, scalar2=1.0 / (2 * PI), op0=ALU.add, op1=ALU.mult), ["th"], ["kf"])
        dv(lambda e: e.tensor_copy(out=ki, in_=kf), ["kf"], ["ki"])
        dv(lambda e: e.tensor_copy(out=kf, in_=ki), ["ki"], ["kf"])
        dv(lambda e: e.tensor_scalar(out=kf, in0=kf, scalar1=-2 * PI, scalar2=shift, op0=ALU.mult, op1=ALU.add), ["kf"], ["kf"])
        tt_op("dve", ang, th, kf, ALU.add, ["th", "kf"], ["ang"])
        dv(lambda e: e.tensor_scalar(out=fix, in0=ang, scalar1=PI, scalar2=-2 * PI, op0=ALU.is_gt, op1=ALU.mult), ["ang"], ["fix"])
        tt_op("dve", ang, ang, fix, ALU.add, ["ang", "fix"], ["ang"])
        dv(lambda e: e.tensor_scalar(out=fix, in0=ang, scalar1=-PI, scalar2=2 * PI, op0=ALU.is_lt, op1=ALU.mult), ["ang"], ["fix"])
        tt_op("dve", ang, ang, fix, ALU.add, ["ang", "fix"], ["ang"])
        P.add("act", lambda e: e.activation(out=dst, in_=ang, func=AF.Sin), reads=["ang"], writes=[name])

    range_reduce_sin(sin1, 0.0, "sin1")
    range_reduce_sin(cos1, PI / 2, "cos1")
    P.add("act", lambda e: e.activation(out=mag1, in_=lrdt, func=AF.Exp), reads=["lrdt"], writes=["mag1"])

    def K(tabv, k):
        return tabv[:, k * 128:(k + 1) * 128]

    dv(lambda e: e.memset(K(crk, 0), 1.0), [], ["crk"])
    dv(lambda e: e.memset(K(cik, 0), 0.0), [], ["cik"])
    tt_op("dve", K(crk, 1), mag1, cos1, ALU.mult, ["mag1", "cos1", "crk"], ["crk"])
    tt_op("dve", K(cik, 1), mag1, sin1, ALU.mult, ["mag1", "sin1", "cik"], ["cik"])
    for k in range(2, 9):
        tt_op("dve", ta, K(crk, k - 1), K(crk, 1), ALU.mult, ["crk"], ["ta"])
        tt_op("dve", tb, K(cik, k - 1), K(cik, 1), ALU.mult, ["cik"], ["tb"])
        tt_op("dve", K(crk, k), ta, tb, ALU.subtract, ["ta", "tb", "crk"], ["crk"])
        tt_op("dve", ta, K(crk, k - 1), K(cik, 1), ALU.mult, ["crk", "cik"], ["ta"])
        tt_op("dve", tb, K(cik, k - 1), K(crk, 1), ALU.mult, ["crk", "cik"], ["tb"])
        tt_op("dve", K(cik, k), ta, tb, ALU.add, ["ta", "tb", "cik"], ["cik"])
    for k in range(8):
        P.add("act", (lambda k: lambda e: e.activation(out=mN, in_=lrdt, func=AF.Exp, scale=-2.0 * k))(k), reads=["lrdt", "crN", "ciN"], writes=["mN"])
        tt_op("dve", K(crN, k), K(crk, k), mN, ALU.mult, ["crk", "mN"], ["crN"])
        dv((lambda k: lambda e: e.scalar_tensor_tensor(out=K(ciN, k), in0=K(cik, k), scalar=-1.0, in1=mN, op0=ALU.mult, op1=ALU.mult))(k), ["cik", "mN"], ["ciN"])
    tt_op("dve", ta, lr, lr, ALU.mult, ["lr"], ["ta"])
    tt_op("dve", tb, lamim, lamim, ALU.mult, ["lamim"], ["tb"])
    tt_op("dve", den, ta, tb, ALU.add, ["ta", "tb"], ["den"])
    dv(lambda e: e.reciprocal(out=den, in_=den), ["den"], ["den"])
    dv(lambda e: e.tensor_scalar(out=ta, in0=K(crk, 1), scalar1=-1.0, scalar2=None, op0=ALU.add), ["crk"], ["ta"])
    tt_op("dve", tb, ta, lr, ALU.mult, ["ta", "lr"], ["tb"])
    tt_op("dve", wr, K(cik, 1), lamim, ALU.mult, ["cik", "lamim"], ["wr"])
    tt_op("dve", wr, wr, tb, ALU.add, ["wr", "tb"], ["wr"])
    tt_op("dve", wr, wr, den, ALU.mult, ["wr", "den"], ["wr"])
    tt_op("dve", tb, ta, lamim, ALU.mult, ["ta", "lamim"], ["tb"])
    tt_op("dve", wi, K(cik, 1), lr, ALU.mult, ["cik", "lr"], ["wi"])
    tt_op("dve", wi, wi, tb, ALU.subtract, ["wi", "tb"], ["wi"])
    tt_op("dve", wi, wi, den, ALU.mult, ["wi", "den"], ["wi"])
    a8v = a8tab[:].rearrange("p l g f -> p (l g) f")
    dv(lambda e: e.tensor_copy(out=a8v[:, :, 0], in_=K(crk, 8)), ["crk"], ["a8tab"])
    dv(lambda e: e.tensor_copy(out=a8v[:, :, 1], in_=K(crk, 8)), ["crk", "a8tab"], ["a8tab"])
    dv(lambda e: e.tensor_scalar(out=a8v[:, :, 2], in0=K(cik, 8), scalar1=-1.0, scalar2=None, op0=ALU.mult), ["cik", "a8tab"], ["a8tab"])
    dv(lambda e: e.tensor_copy(out=a8v[:, :, 3], in_=K(cik, 8)), ["cik", "a8tab"], ["a8tab"])

    sre, sim_ = tab("sre", 128), tab("sim", 128)
    dv(lambda e: e.tensor_copy(out=sre, in_=K(crk, 8)), ["crk"], ["sre"])
    dv(lambda e: e.tensor_copy(out=sim_, in_=K(cik, 8)), ["cik"], ["sim"])
    for _ in range(3):
        tt_op("dve", ta, sre, sre, ALU.mult, ["sre"], ["ta"])
        tt_op("dve", tb, sim_, sim_, ALU.mult, ["sim"], ["tb"])
        dv(lambda e: e.scalar_tensor_tensor(out=sim_, in0=sre, scalar=2.0, in1=sim_, op0=ALU.mult, op1=ALU.mult), ["sre", "sim"], ["sim"])
        tt_op("dve", sre, ta, tb, ALU.subtract, ["ta", "tb", "sim"], ["sre"])
    a64v = a64tab[:].rearrange("p l g f -> p (l g) f")
    dv(lambda e: e.tensor_copy(out=a64v[:, :, 0], in_=sre), ["sre"], ["a64tab"])
    dv(lambda e: e.tensor_copy(out=a64v[:, :, 1], in_=sre), ["sre", "a64tab"], ["a64tab"])
    dv(lambda e: e.tensor_scalar(out=a64v[:, :, 2], in0=sim_, scalar1=-1.0, scalar2=None, op0=ALU.mult), ["sim", "a64tab"], ["a64tab"])
    dv(lambda e: e.tensor_copy(out=a64v[:, :, 3], in_=sim_), ["sim", "a64tab"], ["a64tab"])

    def k4(tabv, nk):
        return tabv[:, 0:nk * 128].rearrange("p (k l g) -> p k l g", k=nk, l=2)

    crk4, cik4, crN4, ciN4 = k4(crk, 9), k4(cik, 9), k4(crN, 8), k4(ciN, 8)

    def idx_tab(dst, src4, lo_sl, hi_sl, rd, wrk, l):
        d3 = dst.rearrange("p (k g) -> p k g", k=8)
        dv(lambda e: e.tensor_copy(out=d3[0:64], in_=src4[0:64, lo_sl, l, :]), [rd, wrk], [wrk])
        dv(lambda e: e.tensor_copy(out=d3[64:128], in_=src4[64:128, hi_sl, l, :]), [rd, wrk], [wrk])

    rev7 = slice(7, None, -1)

    hkeys = ["bbr", "bbi", "BTre", "BTim", "Rre", "Rimn", "CTre", "CTimn", "tq", "tq2", "gcs", "zc"]
    pro_keys += hkeys

    def bc_gh(v, l, mu, width):
        base = (mu * 8) * width
        return v[:, base:base + 8 * width].rearrange("p (g h) -> p g h", g=8).unsqueeze(2).to_broadcast([128, 8, 8, width])

    def bc_kg(v, l, mu):
        vv = v.rearrange("p (k g) -> p g k", k=8)[:, mu * 8:(mu + 1) * 8, :]
        return vv.unsqueeze(3).to_broadcast([128, 8, 8, 16])

    def v4(v):
        return v.rearrange("p (g k h) -> p g k h", g=8, k=8)

    def cplx_tab(dre, dim_, are, aim, l, mu, cr_t, ci_t, neg_im, tagre, tagim, e1, e2):
        A_re, A_im = are, aim
        tt_op(e1, v4(dre), A_re, bc_kg(cr_t, l, mu), ALU.mult, ["Bre_in", "Bim_in", "Cre_in", "Cim_in", "bbr", "bbi", "crB", "crC", "crR"], [tagre])
        tt_op(e1, v4(tq), A_im, bc_kg(ci_t, l, mu), ALU.mult, ["Bre_in", "Bim_in", "Cre_in", "Cim_in", "bbr", "bbi", "ciB", "ciC", "ciR"], ["tq"])
        tt_op(e1, dre, dre, tq, ALU.subtract, [tagre, "tq"], [tagre])
        tt_op(e2, v4(dim_), A_re, bc_kg(ci_t, l, mu), ALU.mult, ["Bre_in", "Bim_in", "Cre_in", "Cim_in", "bbr", "bbi", "ciB", "ciC", "ciR"], [tagim])
        tt_op(e2, v4(tq2), A_im, bc_kg(cr_t, l, mu), ALU.mult, ["Bre_in", "Bim_in", "Cre_in", "Cim_in", "bbr", "bbi", "crB", "crC", "crR"], ["tq2"])
        if neg_im:
            P.add("dve", lambda e: e.scalar_tensor_tensor(out=dim_, in0=dim_, scalar=-1.0, in1=tq2, op0=ALU.mult, op1=ALU.subtract),
                  reads=[tagim, "tq2"], writes=[tagim])
        else:
            tt_op(e2, dim_, dim_, tq2, ALU.add, [tagim, "tq2"], [tagim])

    wmode = {"mode": "plain", "tid": 0}

    def wload(src3, nk, width=128):
        s = counters["ws"] % NST
        counters["ws"] += 1
        dst = wst[s][:, 0:nk * width].rearrange("p (k c) -> p k c", k=nk)
        if wmode["mode"] == "reuse":
            tid = wmode["tid"]
            wmode["tid"] += 1
            P.dma("pool", wst[s][:, 0:nk * width], wbf_d[tid][:, 0:nk * width], reads=[("wbf", tid)], writes=[("ws", s)])
            return s, dst
        P.dma("pool", dst, src3, writes=[("ws", s)])
        if wmode["mode"] == "save":
            tid = wmode["tid"]
            wmode["tid"] += 1
            P.dma("sp", wbf_d[tid][:, 0:nk * width], wst[s][:, 0:nk * width], reads=[("ws", s)], writes=[("wbf", tid)])
        return s, dst

    MODBANK = {0: 4, 1: 5}

    def mod_tile(l, m):
        wav = w_ada[l].rearrange("(k p) c -> p k c", p=128)
        psM = bank(MODBANK[l])
        s, wv = wload(wav[:, :, m * 128:(m + 1) * 128], 16)

        def mm(e, wv=wv, m=m, psM=psM):
            ins = None
            for kc in range(NKC):
                ins = e.matmul(psM[:, m * 2:m * 2 + 2], lhsT=wv[:, kc, :], rhs=scond[:, kc, :], start=(kc == 0), stop=(kc == NKC - 1))
            return ins
        P.add("pe", mm, reads=[("ws", s), "scond"], writes=[("ps", MODBANK[l])])

    def mod_finish(l):
        psM = bank(MODBANK[l])
        tt_op("dve", modT[:, l], psM[:, 0:192].rearrange("p (m g) -> p m g", g=2),
              b_adaT[:, l].unsqueeze(2).to_broadcast([128, 96, 2]), ALU.add, [("ps", MODBANK[l]), "b_adaT"], ["modT"])
        for n, (gT, gk, scoff) in enumerate(((gmixT, "gmixT", 16), (gffnT, "gffnT", 64))):
            P.add("dve", lambda e, l=l, n=n, gT=gT, scoff=scoff: e.scalar_tensor_tensor(
                out=Atab[:, l, n], in0=modT[:, l, scoff:scoff + 16, :], scalar=1.0,
                in1=gT[:, l].unsqueeze(2).to_broadcast([128, 16, 2]), op0=ALU.add, op1=ALU.mult),
                reads=["modT", gk], writes=["Atab"])

    mod_list = [(l, m) for l in range(2) for m in range(96)]

    for l in range(2):
        P.dma("sp", Bre_in, Bre_d[:, l * 1024:(l + 1) * 1024], writes=["Bre_in"])
        P.dma("sp", Bim_in, Bim_d[:, l * 1024:(l + 1) * 1024], writes=["Bim_in"])
        P.dma("sp", Cre_in, Cre_d[:, l * 1024:(l + 1) * 1024], writes=["Cre_in"])
        P.dma("sp", Cim_in, Cim_d[:, l * 1024:(l + 1) * 1024], writes=["Cim_in"])
        idx_tab(crB, crk4, rev7, slice(0, 8), "crk", "crB", l)
        idx_tab(ciB, cik4, rev7, slice(0, 8), "cik", "ciB", l)
        idx_tab(crC, crk4, slice(1, 9), slice(8, 0, -1), "crk", "crC", l)
        idx_tab(ciC, cik4, slice(1, 9), slice(8, 0, -1), "cik", "ciC", l)
        idx_tab(crR, crN4, rev7, slice(0, 8), "crN", "crR", l)
        idx_tab(ciR, ciN4, rev7, slice(0, 8), "ciN", "ciR", l)
        for mu in range(8):
            it = l * 8 + mu
            sg = stg[it % 2]
            sgk = "stg%d" % (it % 2)
            sg5 = sg.rearrange("p (g s c) -> p g s c", g=8, s=5)
            b0 = (l * 64 + mu * 8)
            wrb = wr[:, b0:b0 + 8].unsqueeze(2).to_broadcast([128, 8, 16])
            wib = wi[:, b0:b0 + 8].unsqueeze(2).to_broadcast([128, 8, 16])
            Br = Bre_in[:, mu * 128:(mu + 1) * 128].rearrange("p (g h) -> p g h", g=8)
            Bi = Bim_in[:, mu * 128:(mu + 1) * 128].rearrange("p (g h) -> p g h", g=8)
            bbr3 = bbr.rearrange("p (g h) -> p g h", g=8)
            bbi3 = bbi.rearrange("p (g h) -> p g h", g=8)
            tq3 = tq[:, 0:128].rearrange("p (g h) -> p g h", g=8)
            tt_op("dve", bbr3, Br, wrb, ALU.mult, ["Bre_in", "wr", "BTre", "BTim"], ["bbr"])
            tt_op("dve", tq3, Bi, wib, ALU.mult, ["Bim_in", "wi"], ["tq"])
            tt_op("dve", bbr3, bbr3, tq3, ALU.subtract, ["bbr", "tq"], ["bbr"])
            tt_op("dve", bbi3, Bi, wrb, ALU.mult, ["Bim_in", "wr", "BTre", "BTim"], ["bbi"])
            tt_op("dve", tq3, Br, wib, ALU.mult, ["Bre_in", "wi"], ["tq"])
            tt_op("dve", bbi3, bbi3, tq3, ALU.add, ["bbi", "tq"], ["bbi"])
            bbr_b = bbr3.unsqueeze(2).to_broadcast([128, 8, 8, 16])
            bbi_b = bbi3.unsqueeze(2).to_broadcast([128, 8, 8, 16])
            cplx_tab(BTre, BTim, bbr_b, bbi_b, l, mu, crB, ciB, False, "BTre", "BTim", "dve", "dve")
            cplx_tab(Rre, Rimn, bc_gh(Cre_in, l, mu, 16), bc_gh(Cim_in, l, mu, 16), l, mu, crR, ciR, True, "Rre", "Rimn", "dve", "dve")
            cplx_tab(CTre, CTimn, bc_gh(Cre_in, l, mu, 16), bc_gh(Cim_in, l, mu, 16), l, mu, crC, ciC, True, "CTre", "CTimn", "dve", "dve")
            P.add("act", lambda e, sg5=sg5: e.copy(out=sg5[:, :, 3, :], in_=CTre.rearrange("p (g c) -> p g c", g=8)),
                  reads=["CTre"], writes=[sgk])
            P.add("act", lambda e, sg5=sg5: e.copy(out=sg5[:, :, 4, :], in_=CTimn.rearrange("p (g c) -> p g c", g=8)),
                  reads=["CTimn", sgk], writes=[sgk])
            for half in range(2):
                gsl = slice(half * 4, half * 4 + 4)
                def tr(e, half=half):
                    ins = None
                    for gi in range(4):
                        g8 = half * 4 + gi
                        e.transpose(out=psum[:, (gi * 2) * 128:(gi * 2 + 1) * 128], in_=BTre[:, g8 * 128:(g8 + 1) * 128], identity=identf[:])
                        ins = e.transpose(out=psum[:, (gi * 2 + 1) * 128:(gi * 2 + 2) * 128], in_=BTim[:, g8 * 128:(g8 + 1) * 128], identity=identf[:])
                    return ins
                P.add("pe", tr, reads=["BTre", "BTim", "identf"], writes=[("ps", 0), ("ps", 1)])
                for bb in range(2):
                    g0_ = half * 4 + bb * 2
                    P.add("act", lambda e, sg5=sg5, g0_=g0_, bb=bb: e.copy(out=sg5[:, g0_:g0_ + 2, 0:2, :],
                                                                        in_=bank(bb).rearrange("p (g s c) -> p g s c", g=2, s=2)),
                          reads=[("ps", bb), sgk], writes=[sgk])
                def tp(e, half=half):
                    ins = None
                    for gi in range(4):
                        g8 = half * 4 + gi
                        cs = slice(g8 * 128, (g8 + 1) * 128)
                        for d in range(2):
                            ps_ = psum[:, (2 + d) * 512 + gi * 128:(2 + d) * 512 + (gi + 1) * 128]
                            rows = slice(d * 64, d * 64 + 64)
                            e.matmul(ps_, lhsT=BTre[rows, cs], rhs=Rre[rows, cs], start=True, stop=False)
                            ins = e.matmul(ps_, lhsT=BTim[rows, cs], rhs=Rimn[rows, cs], start=False, stop=True)
                    return ins
                P.add("pe", tp, reads=["BTre", "BTim", "Rre", "Rimn"], writes=[("ps", 2), ("ps", 3)])
                mf_b = maskf[:].unsqueeze(1).to_broadcast([128, 4, 128])
                mb_b = maskb[:].unsqueeze(1).to_broadcast([128, 4, 128])
                id_b = identf[:].unsqueeze(1).to_broadcast([128, 4, 128])
                tqa = tq[:, 0:512].rearrange("p (g c) -> p g c", g=4)
                tqb = tq2[:, 0:512].rearrange("p (g c) -> p g c", g=4)
                tt_op("dve", tqa, bank(2).rearrange("p (g c) -> p g c", g=4), mf_b, ALU.mult, [("ps", 2), "maskf"], ["tq"])
                tt_op("dve", tqb, bank(3).rearrange("p (g c) -> p g c", g=4), mb_b, ALU.mult, [("ps", 3), "maskb"], ["tq2"])
                tt_op("dve", tqa, tqa, tqb, ALU.add, ["tq", "tq2"], ["tq"])
                dcol = dtab[:, b0 + half * 4:b0 + half * 4 + 4].unsqueeze(2).to_broadcast([128, 4, 128])
                tt_op("dve", tqb, id_b, dcol, ALU.mult, ["identf", "dtab"], ["tq2"])
                tt_op("dve", sg5[:, gsl, 2, :], tqa, tqb, ALU.add, ["tq", "tq2", sgk], [sgk])
            P.dma("sp", ssmw_d[l, mu], sg, reads=[sgk], writes=[("ssmw_d", l, mu)])
            for (ml_, mm_) in mod_list[it * 12:(it + 1) * 12]:
                mod_tile(ml_, mm_)
    mod_finish(0)
    mod_finish(1)

    main_keys = [("xT", kc, tt) for kc in range(NKC) for tt in range(2)] + [("hT", kc, tt) for kc in range(NKC) for tt in range(2)] + \
                [("mix", j, tt) for j in range(NKC) for tt in range(2)]
    dummy = sb("barrier_dummy", [128, 2], F32)
    P.add("dve", lambda e: e.memset(dummy[:], 0.0), reads=pro_keys + ["stg0", "stg1"], writes=main_keys + ["gcs", "zc", "acc", ("rstd", 0), ("rstd", 1)])

    def run_pass(grp, NT, x_d, y_d, seqs):
        TT = NT // 512
        NC = NT // 8
        nseq = len(seqs)
        NCs = NC // nseq
        SB = NCs + 1
        tts = list(range(TT))

        def xk(kc, tt):
            return ("xT", kc, tt)

        def hk(kc, tt):
            return ("hT", kc, tt)

        def mk(j, tt):
            return ("mix", j, tt)

        def tsl(tt):
            return slice(tt * 512, (tt + 1) * 512)

        xv = x_d.rearrange("(k p) t -> p k t", p=128)
        for kc in range(NKC):
            P.dma("sp", xT[:, kc, 0:NT], xv[:, kc, :], writes=[xk(kc, tt) for tt in tts])

        def norm(l, n, final=False):
            for tt in tts:
                for kc in range(NKC):
                    si = counters["sq"] % 2
                    counters["sq"] += 1
                    P.add("act", lambda e, si=si, kc=kc, tt=tt: e.activation(out=sq[si][:], in_=xT[:, kc, tsl(tt)], func=AF.Square),
                          reads=[xk(kc, tt)], writes=[("sq", si)])
                    P.add("pe", lambda e, si=si, kc=kc, tt=tt: e.matmul(bank(4 + tt), lhsT=ones_bf[:], rhs=sq[si][:], start=(kc == 0), stop=(kc == NKC - 1)),
                          reads=[("sq", si), "ones_bf"], writes=[("ps", 4 + tt)])
                P.add("act", lambda e, tt=tt: e.activation(out=rstd[:, tsl(tt)], in_=bank(4 + tt), func=AF.Sqrt, bias=epsc[:, 0:1], scale=1.0 / D),
                      reads=[("ps", 4 + tt), "epsc", "zc"], writes=[("rstd", tt), "zc"])
                P.add("dve", lambda e, tt=tt: e.reciprocal(out=rstd[:, tsl(tt)], in_=rstd[:, tsl(tt)]), reads=[("rstd", tt), "zc"], writes=[("rstd", tt), "zc"])
                for kc in range(NKC):
                    ti = counters["tmp"] % 2
                    counters["tmp"] += 1
                    P.add("dve", lambda e, ti=ti, kc=kc, tt=tt: e.tensor_tensor(out=tmpf[ti][:], in0=xT[:, kc, tsl(tt)], in1=rstd[:, tsl(tt)], op=ALU.mult),
                          reads=[xk(kc, tt), ("rstd", tt), "zc"], writes=[("tmpf", ti)])
                    if not final:
                        shoff = 0 if n == 0 else 48
                        P.add("act", lambda e, ti=ti, kc=kc, tt=tt, shoff=shoff: e.activation(
                            out=hT[:, kc, tsl(tt)], in_=tmpf[ti][:], func=AF.Identity,
                            bias=modT[:, l, shoff + kc, grp:grp + 1], scale=Atab[:, l, n, kc, grp:grp + 1]),
                            reads=[("tmpf", ti), "modT", "Atab"], writes=[hk(kc, tt)])
                    else:
                        so = (kc * TT + tt) % 16
                        ostg = hflat[:, so * 512:(so + 1) * 512]
                        okeys = [("hT", so, 0), ("hT", so, 1)]
                        P.add("act", lambda e, ti=ti, ostg=ostg, kc=kc: e.activation(
                            out=ostg, in_=tmpf[ti][:], func=AF.Identity, scale=gfinT[:, kc:kc + 1]),
                            reads=[("tmpf", ti), "gfinT"] + okeys, writes=okeys)
                        P.dma("sp", y_d.rearrange("(k p) t -> p k t", p=128)[:, kc, tsl(tt)], ostg,
                              reads=okeys, is_output=True)

        def big_mm(wv, nk, rhs_fn, rhs_keys_fn, s, pb):
            for tt in tts:
                def mm(e, tt=tt):
                    ins = None
                    for k in range(nk):
                        ins = e.matmul(bank(pb + tt), lhsT=wv[:, k, :], rhs=rhs_fn(k, tt), start=(k == 0), stop=(k == nk - 1))
                    return ins
                P.add("pe", mm, reads=[("ws", s)] + [rhs_keys_fn(k, tt) for k in range(nk)], writes=[("ps", pb + tt)])

        pbc = [0]

        def next_pb():
            pb = (pbc[0] % 2) * 2
            pbc[0] += 1
            return pb

        def h_rhs(k, tt):
            return hT[:, k, tsl(tt)]

        NCt = NC
        L8 = 8
        NBs = NCs // L8
        NB = NCt // L8
        CS = NBs + 1
        GB = 2048 // NC
        MB = GB // 8
        NBATCH = 64 // GB
        xs_words = GB * NCt * 2
        XSflat = mix[:, 8:16, :].rearrange("p a b -> p (a b)").bitcast(F32)[:, 0:xs_words]
        XS = XSflat.rearrange("p (g c r) -> p g c r", g=GB, r=2)
        XB = XSflat.rearrange("p (g b j r) -> p g b j r", g=GB, j=L8, r=2)
        xs_keys = ["XS"] + [("mix", j, tt) for j in range(8, 16) for tt in tts]
        nB2 = GB * NB * 2
        assert nB2 <= 512
        t1 = gcs[:, 0:nB2].rearrange("p (g b r) -> p g b r", g=GB, r=2)
        t2 = gcs[:, 512:512 + nB2].rearrange("p (g b r) -> p g b r", g=GB, r=2)
        tmpA = zc[:, 0:nB2].rearrange("p (g b r) -> p g b r", g=GB, r=2)
        tmpB = zc[:, 512:512 + nB2].rearrange("p (g b r) -> p g b r", g=GB, r=2)
        c1 = gcs[:, 0:GB * nseq * 2].rearrange("p (g s r) -> p g s r", g=GB, r=2)
        c2 = gcs[:, 512:512 + GB * nseq * 2].rearrange("p (g s r) -> p g s r", g=GB, r=2)
        assert GB * nseq * CS * 2 <= 640
        carr = carry[:, 0:GB * nseq * CS * 2].rearrange("p (g c r) -> p g c r", g=GB, r=2)
        tkeys = ["gcs", "zc", "acc", ("rstd", 0), ("rstd", 1)]

        def chk(n, l):
            if stop == ("ssm%d" % n, grp, l):
                raise StopBuild()

        def ush_v(ml, buf):
            return Ush[:, buf].rearrange("p g c -> p (g c)")[:, ml * 8 * NC:(ml + 1) * 8 * NC].rearrange("p (g c) -> p g c", c=NC)

        Ysb_v = Ysb[:].rearrange("p g c -> p (g c)")[:, 0:8 * NC].rearrange("p (g c) -> p g c", c=NC)

        def ssm_preA(l, mu, ml, buf):
            psU = psum[:, 4 * 512:6 * 512]

            def shuf(e):
                ins = None
                for glo in range(4):
                    for tau in range(8):
                        for rb in range(2):
                            ins = e.matmul(psU[:, rb * 512 + glo * NC:rb * 512 + (glo + 1) * NC], lhsT=selA[64 * rb:64 * rb + 64, glo, tau, :],
                                           rhs=u_sb[64 * rb:64 * rb + 64, tau:NT:8], start=(tau == 0), stop=(tau == 7))
                return ins
            P.add("pe", shuf, reads=["u_sb", "selA"], writes=[("ps", 4), ("ps", 5)])
            ushf = Ush[:, buf].rearrange("p g c -> p (g c)")
            for hb in range(2):
                P.add("act", lambda e, hb=hb: e.copy(out=ushf[:, ml * 8 * NC + hb * 4 * NC:ml * 8 * NC + (hb + 1) * 4 * NC], in_=psU[:, hb * 512:hb * 512 + 4 * NC]),
                      reads=[("ps", 4 + hb)], writes=[("Ush", buf, ml)])

        def load_B(l, mu):
            P.dma("sp", ssmwB[:], ssmw_d[l, mu].rearrange("p (g s c) -> p g s c", g=8, s=5)[:, :, 0:2, :],
                  reads=[("ssmw_d", l, mu)], writes=["ssmwB"])

        def load_C(l, mu):
            P.dma("sp", ssmwC[:], ssmw_d[l, mu].rearrange("p (g s c) -> p g s c", g=8, s=5)[:, :, 2:5, :],
                  reads=[("ssmw_d", l, mu)], writes=["ssmwC"])

        def ssm_preB(l, mu, ml, buf):
            if ml > 0:
                load_B(l, mu)
            uv = ush_v(ml, buf)
            for rnd in range(2):
                psS = psum[:, 6 * 512:6 * 512 + 4 * 2 * NC]

                def bmm(e, rnd=rnd, psS=psS):
                    ins = None
                    for gi in range(4):
                        g8 = rnd * 4 + gi
                        for r in range(2):
                            ins = e.matmul(psS[:, (gi * 2 + r) * NC:(gi * 2 + r + 1) * NC], lhsT=ssmwB[:, g8, r, :], rhs=uv[:, g8, :],
                                           start=True, stop=True)
                    return ins
                P.add("pe", bmm, reads=["ssmwB", ("Ush", buf, ml)], writes=[("ps", 6), ("ps", 7)])
                psS4 = psS.rearrange("p (g r c) -> p g r c", g=4, r=2)
                g0 = ml * 8 + rnd * 4
                for sq_i in range(nseq):
                    c0 = sq_i * NCs
                    P.add("act", lambda e, g0=g0, c0=c0, psS4=psS4: e.copy(
                        out=XS[0:64, g0:g0 + 4, c0:c0 + NCs, :].rearrange("p g c r -> p g r c"),
                        in_=psS4[0:64, :, :, c0:c0 + NCs]), reads=[("ps", 6), ("ps", 7), "XSclaim"], writes=[("XSe", ml, rnd, sq_i, 0)])
                    P.add("dve", lambda e, g0=g0, c0=c0, psS4=psS4: e.tensor_copy(
                        out=XS[64:128, g0:g0 + 4, c0:c0 + NCs, :].rearrange("p g c r -> p g r c"),
                        in_=psS4[64:128, :, :, c0 + NCs - 1::-1][:, :, :, 0:NCs] if c0 > 0 else psS4[64:128, :, :, NCs - 1::-1]),
                        reads=[("ps", 6), ("ps", 7), "XSclaim"], writes=[("XSe", ml, rnd, sq_i, 1)])

        def cmul(dst, src, src_sw, A1, A2, ta_, tb_, rk, wk):
            tt_op("dve", ta_, src, A1, ALU.mult, rk, tkeys)
            tt_op("dve", tb_, src_sw, A2, ALU.mult, rk + tkeys, tkeys)
            tt_op("dve", dst, ta_, tb_, ALU.add, tkeys + wk, wk)

        evac_keys = [("XSe", ml_, rnd_, sq_, hf_) for ml_ in range(MB) for rnd_ in range(2) for sq_ in range(nseq) for hf_ in range(2)]

        def scan(l, pair):
            P.add("dve", lambda e: e.memset(dummy[:], 0.0), reads=evac_keys + xs_keys + ["XSclaim"], writes=xs_keys)
            gs16 = slice(pair * GB, pair * GB + GB)
            A1b = a8tab[:, l, gs16, 0:2].unsqueeze(2).to_broadcast([128, GB, NB, 2])
            A2b = a8tab[:, l, gs16, 2:4].unsqueeze(2).to_broadcast([128, GB, NB, 2])
            B1 = a64tab[:, l, gs16, 0:2].unsqueeze(2).to_broadcast([128, GB, nseq, 2])
            B2 = a64tab[:, l, gs16, 2:4].unsqueeze(2).to_broadcast([128, GB, nseq, 2])
            for j in range(1, L8):
                cmul(t1, XB[:, :, :, j - 1, :], XB[:, :, :, j - 1, ::-1], A1b, A2b, t1, t2, xs_keys + ["a8tab"], tkeys)
                tt_op("dve", XB[:, :, :, j, :], XB[:, :, :, j, :], t1, ALU.add, xs_keys + tkeys, xs_keys)
            if grp == 0:
                P.add("dve", lambda e: e.memset(carr[:, :, 0:(nseq - 1) * CS + 1:CS, :], 0.0), reads=["carry"], writes=["carry"])
            else:
                P.add("dve", lambda e: e.tensor_copy(out=carr[:, :, 0, :], in_=sinit[:, l, gs16, :]), reads=["sinit", "carry"], writes=["carry"])
            for b in range(NBs):
                cprev = carr[:, :, b:b + (nseq - 1) * CS + 1:CS, :]
                cprev_sw = carr[:, :, b:b + (nseq - 1) * CS + 1:CS, ::-1]
                cnext = carr[:, :, b + 1:b + 1 + (nseq - 1) * CS + 1:CS, :]
                xfin = XB[:, :, b:b + (nseq - 1) * NBs + 1:NBs, L8 - 1, :]
                cmul(c1, cprev, cprev_sw, B1, B2, c1, c2, ["carry", "a64tab"], tkeys)
                tt_op("dve", cnext, c1, xfin, ALU.add, tkeys + xs_keys + ["carry"], ["carry"])
            cin = carr[:, :, 0:NB, :]
            cur = None
            for j in range(L8):
                dstt = tmpA if j % 2 == 0 else tmpB
                if j == 0:
                    if nseq == 1:
                        src, src_sw = cin, carr[:, :, 0:NB, ::-1]
                        cmul(dstt, src, src_sw, A1b, A2b, t1, t2, ["carry", "a8tab"], tkeys)
                    else:
                        for sq_i in range(nseq):
                            bs = slice(sq_i * NBs, (sq_i + 1) * NBs)
                            a1s = a8tab[:, l, gs16, 0:2].unsqueeze(2).to_broadcast([128, GB, NBs, 2])
                            a2s = a8tab[:, l, gs16, 2:4].unsqueeze(2).to_broadcast([128, GB, NBs, 2])
                            cmul(dstt[:, :, bs, :], carr[:, :, sq_i * CS:sq_i * CS + NBs, :], carr[:, :, sq_i * CS:sq_i * CS + NBs, ::-1],
                                 a1s, a2s, t1[:, :, bs, :], t2[:, :, bs, :], ["carry", "a8tab"], tkeys)
                else:
                    cmul(dstt, cur, cur[:, :, :, ::-1], A1b, A2b, t1, t2, ["a8tab"] + tkeys, tkeys)
                cur = dstt
                tt_op("dve", XB[:, :, :, j, :], XB[:, :, :, j, :], cur, ALU.add, xs_keys + tkeys, xs_keys)
            if grp == 0:
                P.add("act", lambda e: e.copy(out=nst_sb[:, :, l, gs16, :].rearrange("p s g r -> p g s r"),
                                              in_=carr[:, :, NBs:NBs + (nseq - 1) * CS + 1:CS, :]),
                      reads=["carry"], writes=["nst_sb"])

        NHB = (8 * NC + 511) // 512
        YSK = [("Ysb", hb) for hb in range(NHB)]

        def evac_y_top():
            psY_ = psum[:, 4 * 512:6 * 512]
            ysf_ = Ysb[:].rearrange("p g c -> p (g c)")
            for hb in range(NHB):
                if hb == 0:
                    P.add("act", lambda e, hb=hb: e.copy(out=ysf_[:, hb * 512:(hb + 1) * 512], in_=psY_[:, hb * 512:(hb + 1) * 512]),
                          reads=[("ps", 4 + hb)], writes=[("Ysb", hb)])
                else:
                    P.add("dve", lambda e, hb=hb: e.tensor_copy(out=ysf_[:, hb * 512:(hb + 1) * 512], in_=psY_[:, hb * 512:(hb + 1) * 512]),
                          reads=[("ps", 4 + hb)], writes=[("Ysb", hb)])

        def ssm_post(l, mu, ml, buf, phase="all"):
            psY = psum[:, 4 * 512:6 * 512]
            ysf = Ysb[:].rearrange("p g c -> p (g c)")
            nhb = (8 * NC + 511) // 512
            ysk = [("Ysb", hb) for hb in range(nhb)]

            def evac_y():
                for hb in range(nhb):
                    if hb == 0:
                        P.add("act", lambda e, hb=hb: e.copy(out=ysf[:, hb * 512:(hb + 1) * 512], in_=psY[:, hb * 512:(hb + 1) * 512]),
                              reads=[("ps", 4 + hb)], writes=[("Ysb", hb)])
                    else:
                        P.add("dve", lambda e, hb=hb: e.tensor_copy(out=ysf[:, hb * 512:(hb + 1) * 512], in_=psY[:, hb * 512:(hb + 1) * 512]),
                              reads=[("ps", 4 + hb)], writes=[("Ysb", hb)])

            if phase in ("all", "a"):
                post_a(l, mu, ml, buf)
                if phase == "all" or ml == 0:
                    evac_y()
            if phase in ("all", "b"):
                if phase == "b" and ml > 0:
                    evac_y()
                post_b(l, mu, ysk)

        def post_a(l, mu, ml, buf):
            if ml > 0:
                load_C(l, mu)
            gsl8 = slice(ml * 8, ml * 8 + 8)
            for sq_i in range(nseq):
                c0 = sq_i * NCs
                cb = sq_i * CS
                P.add("act", lambda e, c0=c0: e.copy(out=Xbf[0:64, :, :, c0 + 1:c0 + NCs],
                                                     in_=XS[0:64, gsl8, c0:c0 + NCs - 1, :].rearrange("p g c r -> p g r c")),
                      reads=xs_keys + ["Xbf"], writes=["Xbf"])
                P.add("act", lambda e, c0=c0, cb=cb: e.copy(out=Xbf[0:64, :, :, c0], in_=carr[0:64, gsl8, cb, :]),
                      reads=["carry", "Xbf"], writes=["Xbf"])
                P.add("dve", lambda e, c0=c0: e.tensor_copy(
                    out=Xbf[64:128, :, :, c0:c0 + NCs - 1],
                    in_=(XS[64:128, gsl8, c0 + NCs - 2::-1, :][:, :, 0:NCs - 1, :] if c0 > 0 else XS[64:128, gsl8, NCs - 2::-1, :]).rearrange("p g c r -> p g r c")),
                    reads=xs_keys + ["XbfB"], writes=["XbfB"])
                P.add("dve", lambda e, c0=c0, cb=cb: e.tensor_copy(out=Xbf[64:128, :, :, c0 + NCs - 1], in_=carr[64:128, gsl8, cb, :]),
                      reads=["carry", "XbfB"], writes=["XbfB"])
            uv = ush_v(ml, buf)
            psY = psum[:, 4 * 512:6 * 512]

            def ymm(e):
                ins = None
                for g8 in range(8):
                    o = psY[:, g8 * NC:(g8 + 1) * NC]
                    e.matmul(o, lhsT=ssmwC[:, g8, 0, :], rhs=uv[:, g8, :], start=True, stop=False)
                    for r in range(2):
                        ins = e.matmul(o, lhsT=ssmwC[:, g8, 1 + r, :], rhs=Xbf[:, g8, r, 0:NC], start=False, stop=(r == 1))
                return ins
            P.add("pe", ymm, reads=["ssmwC", "Xbf", "XbfB", ("Ush", buf, ml)], writes=[("ps", 4), ("ps", 5)])

        def post_b(l, mu, ysk):
            for tt in tts:
                def unsh(e, tt=tt):
                    ins = None
                    for tlo in range(4):
                        for g8 in range(8):
                            for tb_ in range(2):
                                t = tb_ * 4 + tlo
                                ob = bank(0 + tt) if tb_ == 0 else bank(2 + tt)
                                ins = e.matmul(ob[:, t:512:8], lhsT=selA[64 * tb_:64 * tb_ + 64, tlo, g8, :],
                                               rhs=Ysb_v[64 * tb_:64 * tb_ + 64, g8, tt * 64:(tt + 1) * 64], start=(g8 == 0), stop=(g8 == 7))
                    return ins
                P.add("pe", unsh, reads=ysk + ["selA"], writes=[("ps", 0 + tt), ("ps", 2 + tt)])
                for hf in range(2):
                    bk = (0 + tt) if hf == 0 else (2 + tt)
                    src = bank(bk).rearrange("p (c t) -> p c t", t=8)[:, :, 4 * hf:4 * hf + 4]
                    dst = mix[:, mu, tsl(tt)].rearrange("p (c t) -> p c t", t=8)[:, :, 4 * hf:4 * hf + 4]
                    P.add("act", lambda e, src=src, dst=dst: e.activation(out=dst, in_=src, func=AF.Gelu_apprx_tanh),
                          reads=[("ps", bk)], writes=[mk(mu, tt)])

        for l in range(2):
            norm(l, 0)
            if stop == ("norm1", grp, l):
                return True
            win = w_in[l].rearrange("(k p) c -> p k c", p=128)
            def pre_A(pair):
                for ml in range(MB):
                    mu = pair * MB + ml
                    s, wv = wload(win[:, :, mu * 128:(mu + 1) * 128], 16)
                    pb = next_pb()
                    big_mm(wv, 16, h_rhs, hk, s, pb)
                    for tt in tts:
                        P.add("act", lambda e, tt=tt, pb=pb: e.copy(out=u_sb[:, tsl(tt)], in_=bank(pb + tt)), reads=[("ps", pb + tt), "u_sb"], writes=["u_sb"])
                    ssm_preA(l, mu, ml, pair % 2)

            def pre_B(pair):
                P.add("dve", lambda e: e.memset(dummy[:], 0.0), reads=xs_keys, writes=xs_keys + ["XSclaim"])
                for ml in range(MB):
                    ssm_preB(l, pair * MB + ml, ml, pair % 2)

            pre_A(0)
            load_B(l, 0)
            pre_B(0)
            for pair in range(NBATCH):
                if pair + 1 < NBATCH:
                    pre_A(pair + 1)
                    load_B(l, (pair + 1) * MB)
                load_C(l, pair * MB)
                chk(2, l)
                scan(l, pair)
                chk(3, l)
                if MB == 2:
                    for ml in range(MB):
                        ssm_post(l, pair * MB + ml, ml, pair % 2, phase="a")
                    if pair + 1 < NBATCH:
                        pre_B(pair + 1)
                    for ml in range(MB):
                        ssm_post(l, pair * MB + ml, ml, pair % 2, phase="b")
                else:
                    buf = pair % 2
                    m0 = pair * MB
                    post_a(l, m0 + 0, 0, buf)
                    evac_y_top()
                    post_a(l, m0 + 1, 1, buf)
                    post_b(l, m0 + 0, YSK)
                    evac_y_top()
                    post_a(l, m0 + 2, 2, buf)
                    post_b(l, m0 + 1, YSK)
                    evac_y_top()
                    post_a(l, m0 + 3, 3, buf)
                    if pair + 1 < NBATCH:
                        pre_B(pair + 1)
                    post_b(l, m0 + 2, YSK)
                    evac_y_top()
                    post_b(l, m0 + 3, YSK)
                chk(5, l)
            if stop == ("ssmall", grp, l):
                return True
            for j in range(8):
                s, wv = wload(win[:, :, 2048 + j * 128:2048 + (j + 1) * 128], 16)
                pb = next_pb()
                big_mm(wv, 16, h_rhs, hk, s, pb)
                for tt in tts:
                    P.add("act", lambda e, tt=tt, pb=pb: e.copy(out=gcs[:, tsl(tt)], in_=bank(pb + tt)), reads=[("ps", pb + tt), "gcs"], writes=["gcs"])
                s, wv = wload(win[:, :, 3072 + j * 128:3072 + (j + 1) * 128], 16)
                pb = next_pb()
                big_mm(wv, 16, h_rhs, hk, s, pb)
                for tt in tts:
                    P.add("dve", lambda e, tt=tt, pb=pb: e.tensor_tensor(out=zc[:, tsl(tt)], in0=gcs[:, tsl(tt)], in1=bank(pb + tt), op=ALU.mult),
                          reads=[("ps", pb + tt), "gcs", "zc", ("rstd", 0), ("rstd", 1)], writes=["zc", ("rstd", 0), ("rstd", 1)])
                P.add("act", lambda e, j=j, l=l: e.activation(out=acc[:, 0:NT], in_=zc[:, 0:NT], func=AF.Identity,
                                                         bias=convb[:, l, j:j + 1], scale=convw[:, l, 1, j:j + 1]),
                      reads=["zc", "convw", "convb", "acc", "gcs"], writes=["acc", "gcs"])
                RL = 64 if grp == 1 else 256
                acc3 = acc[:, 0:NT].rearrange("p (r c) -> p r c", c=RL)
                zc3 = zc[:, 0:NT].rearrange("p (r c) -> p r c", c=RL)
                P.add("dve", lambda e, j=j, acc3=acc3, zc3=zc3, RL=RL, l=l: e.scalar_tensor_tensor(
                    out=acc3[:, :, 1:RL], in0=zc3[:, :, 0:RL - 1], scalar=convw[:, l, 0, j:j + 1], in1=acc3[:, :, 1:RL], op0=ALU.mult, op1=ALU.add),
                    reads=["zc", "convw", "acc", "gcs"], writes=["acc", "gcs"])
                P.add("dve", lambda e, j=j, acc3=acc3, zc3=zc3, RL=RL, l=l: e.scalar_tensor_tensor(
                    out=acc3[:, :, 0:RL - 1], in0=zc3[:, :, 1:RL], scalar=convw[:, l, 2, j:j + 1], in1=acc3[:, :, 0:RL - 1], op0=ALU.mult, op1=ALU.add),
                    reads=["zc", "convw", "acc", "gcs"], writes=["acc", "gcs"])
                s, wv = wload(win[:, :, 1024 + j * 128:1024 + (j + 1) * 128], 16)
                pb = next_pb()
                big_mm(wv, 16, h_rhs, hk, s, pb)
                for tt in tts:
                    P.add("dve", lambda e, tt=tt, pb=pb, j=j: e.tensor_tensor(out=mix[:, 8 + j, tsl(tt)], in0=acc[:, tsl(tt)], in1=bank(pb + tt), op=ALU.mult),
                          reads=[("ps", pb + tt), "acc", "gcs"], writes=[mk(8 + j, tt)])
            if stop == ("mixer", grp, l):
                return True
            wgl = w_glu[l].rearrange("(k p) c -> p k c", p=128)
            for m in range(8):
                s, wv = wload(wgl[:, :, m * 128:(m + 1) * 128], 8)
                pb = next_pb()
                big_mm(wv, 8, lambda k, tt: mix[:, k, tsl(tt)], mk, s, pb)
                for tt in tts:
                    ti = counters["tmp"] % 2
                    counters["tmp"] += 1
                    P.add("act", lambda e, tt=tt, pb=pb, ti=ti: e.activation(out=tmpf[ti][:], in_=bank(pb + tt), func=AF.Sigmoid),
                          reads=[("ps", pb + tt)], writes=[("tmpf", ti)])
                    P.add("dve", lambda e, tt=tt, m=m, ti=ti: e.tensor_tensor(out=hT[:, m, tsl(tt)], in0=tmpf[ti][:], in1=mix[:, m, tsl(tt)], op=ALU.mult),
                          reads=[("tmpf", ti), mk(m, tt)], writes=[hk(m, tt)])
            if stop == ("glu", grp, l):
                return True
            wov = w_out[l].rearrange("(k p) c -> p k c", p=128)
            for dt_ in range(NKC):
                s, wv = wload(wov[:, :, dt_ * 128:(dt_ + 1) * 128], 16)
                pb = next_pb()
                big_mm(wv, 16, lambda k, tt: (hT[:, k, tsl(tt)] if k < 8 else mix[:, k, tsl(tt)]),
                       lambda k, tt: (hk(k, tt) if k < 8 else mk(k, tt)), s, pb)
                for tt in tts:
                    P.add("dve", lambda e, tt=tt, pb=pb, dt_=dt_, l=l: e.scalar_tensor_tensor(
                        out=xT[:, dt_, tsl(tt)], in0=bank(pb + tt), scalar=modT[:, l, 32 + dt_, grp:grp + 1], in1=xT[:, dt_, tsl(tt)],
                        op0=ALU.mult, op1=ALU.add), reads=[("ps", pb + tt), "modT", xk(dt_, tt)], writes=[xk(dt_, tt)])
            if stop == ("wout", grp, l):
                return True
            norm(l, 1)
            wgv = w_gate[l].rearrange("(k p) c -> p k c", p=128)
            wuv = w_up[l].rearrange("(k p) c -> p k c", p=128)
            wdv = w_down[l].rearrange("(j p) c -> p j c", p=128)
            for j0 in range(0, NFF, 16):
                nj = min(16, NFF - j0)
                for jj in range(nj):
                    j = j0 + jj
                    sg_, wg_ = wload(wgv[:, :, j * 128:(j + 1) * 128], 16)
                    su_, wu_ = wload(wuv[:, :, j * 128:(j + 1) * 128], 16)
                    big_mm(wg_, 16, h_rhs, hk, sg_, 0)
                    big_mm(wu_, 16, h_rhs, hk, su_, 2)
                    for tt in tts:
                        ti = counters["tmp"] % 2
                        counters["tmp"] += 1
                        P.add("act", lambda e, tt=tt, ti=ti: e.activation(out=tmpf[ti][:], in_=bank(0 + tt), func=AF.Silu),
                              reads=[("ps", 0 + tt)], writes=[("tmpf", ti)])
                        P.add("dve", lambda e, tt=tt, ti=ti, jj=jj: e.tensor_tensor(out=mix[:, jj, tsl(tt)], in0=tmpf[ti][:], in1=bank(2 + tt), op=ALU.mult),
                              reads=[("tmpf", ti), ("ps", 2 + tt)], writes=[mk(jj, tt)])
                for dt_ in range(NKC):
                    s, wv = wload(wdv[:, j0:j0 + nj, dt_ * 128:(dt_ + 1) * 128], nj)
                    pb = 4 + (dt_ % 2) * 2
                    big_mm(wv, nj, lambda k, tt: mix[:, k, tsl(tt)], mk, s, pb)
                    for tt in tts:
                        P.add("dve", lambda e, tt=tt, pb=pb, dt_=dt_, l=l: e.scalar_tensor_tensor(
                            out=xT[:, dt_, tsl(tt)], in0=bank(pb + tt), scalar=modT[:, l, 80 + dt_, grp:grp + 1], in1=xT[:, dt_, tsl(tt)],
                            op0=ALU.mult, op1=ALU.add), reads=[("ps", pb + tt), "modT", xk(dt_, tt)], writes=[xk(dt_, tt)])
            if stop == ("ffn", grp, l):
                return True
        norm(0, 0, final=True)
        if grp == 0:
            P.dma("sp", nst_d, nst_sb[:].rearrange("p s l g r -> p (s l g r)"), reads=["nst_sb"], is_output=True)

    try:
        wmode["mode"], wmode["tid"] = "save", 0
        if not run_pass(1, 1024, xS, yS, [0]):
            assert wmode["tid"] == NWT, wmode["tid"]
            wmode["mode"], wmode["tid"] = "reuse", 0
            run_pass(0, 512, xP, yP, [0, 1])
    except StopBuild:
        pass

    P.emit()
    for cm in reversed(cms):
        cm.__exit__(None, None, None)
    return nc


_NC_CACHE = {}


def _host_consts():
    ident = np.eye(128, dtype=np.float32)
    tau = np.arange(128) // 16
    mf = (tau[:, None] <= tau[None, :]).astype(np.float32)
    mb = (tau[:, None] >= tau[None, :]).astype(np.float32)
    cf = np.concatenate([ident, mf, mb], axis=1)
    sel = np.zeros((128, 4, 8, 128), np.float32)
    for rb in range(2):
        for a in range(4):
            for h in range(16):
                for t in range(8):
                    sel[64 * rb + a * 16 + h, a, t, t * 16 + h] = 1.0
    return cf, sel.reshape(128, 4096)


def kernel(x_prompt, x_sample, state_ssm, c, c_ctx, w_ada, b_ada, g_mix, w_in,
           ssm_lam_re, ssm_lam_im, ssm_log_dt, ssm_b_re, ssm_b_im, ssm_c_re, ssm_c_im,
           ssm_d, w_glu, conv_w, conv_b, w_out, g_ffn, w_gate, w_up, w_down, g_final):
    f = lambda a: np.ascontiguousarray(np.asarray(a, dtype=np.float32))
    x_prompt, x_sample, state_ssm, c, c_ctx = map(f, (x_prompt, x_sample, state_ssm, c, c_ctx))
    if "nc" not in _NC_CACHE:
        _NC_CACHE["nc"] = build_nc()
    nc = _NC_CACHE["nc"]
    cf, sel = _host_consts()
    shared = {
        "w_ada": f(w_ada), "w_in": f(w_in), "w_glu": f(w_glu), "w_out": f(w_out),
        "w_gate": f(w_gate), "w_up": f(w_up), "w_down": f(w_down),
        "b_adaT": f(np.asarray(b_ada).reshape(2, 96, 128).transpose(2, 0, 1).reshape(128, 192)),
        "gmixT": f(np.asarray(g_mix).reshape(2, 16, 128).transpose(2, 0, 1).reshape(128, 32)),
        "gffnT": f(np.asarray(g_ffn).reshape(2, 16, 128).transpose(2, 0, 1).reshape(128, 32)),
        "gfinT": f(np.asarray(g_final).reshape(16, 128).T),
        "convw": f(np.asarray(conv_w).reshape(2, 3, 8, 128).transpose(3, 0, 1, 2).reshape(128, 48)),
        "convb": f(np.asarray(conv_b).reshape(2, 8, 128).transpose(2, 0, 1).reshape(128, 16)),
        "dtab": f(np.broadcast_to(np.asarray(ssm_d).reshape(2, 64, 16).transpose(2, 0, 1)[None], (8, 16, 2, 64)).reshape(128, 128)),
        "lamre": f(np.asarray(ssm_lam_re).transpose(1, 3, 0, 2).reshape(128, 128)),
        "lamim": f(np.asarray(ssm_lam_im).transpose(1, 3, 0, 2).reshape(128, 128)),
        "logdt": f(np.broadcast_to(np.asarray(ssm_log_dt).transpose(1, 0, 2)[:, None], (2, 64, 2, 64)).reshape(128, 128)),
        "Bre_in": f(np.asarray(ssm_b_re).transpose(1, 3, 0, 2, 4).reshape(128, 2048)),
        "Bim_in": f(np.asarray(ssm_b_im).transpose(1, 3, 0, 2, 4).reshape(128, 2048)),
        "Cre_in": f(np.asarray(ssm_c_re).transpose(1, 4, 0, 2, 3).reshape(128, 2048)),
        "Cim_in": f(np.asarray(ssm_c_im).transpose(1, 4, 0, 2, 3).reshape(128, 2048)),
        "cF32": cf, "selA_f": sel,
    }
    in_maps = []
    for i in range(8):
        m = dict(shared)
        m["xP"] = f(x_prompt[2 * i:2 * i + 2].reshape(512, D).T)
        m["xS"] = f(x_sample[i].T)
        cond = np.stack([c_ctx.reshape(16, 128).T, c[i].reshape(16, 128).T], axis=2)
        m["condT"] = f(cond.reshape(128, 32))
        m["sinit"] = f(state_ssm[i].transpose(1, 4, 0, 3, 2).reshape(128, 256))
        in_maps.append(m)
    res = run_bass_kernel_spmd(nc, in_maps, core_ids=list(range(8)))
    y_prompt = np.empty((16, 256, D), np.float32)
    y_sample = np.empty((8, 1024, D), np.float32)
    new_state = np.empty((16, 2, 2, 2, 64, 64), np.float32)
    for i in range(8):
        r = res.results[i]
        y_prompt[2 * i:2 * i + 2] = np.asarray(r["yP"]).T.reshape(2, 256, D)
        y_sample[i] = np.asarray(r["yS"]).T
        ns = np.asarray(r["nst"]).reshape(2, 64, 2, 2, 64, 2)
        new_state[2 * i:2 * i + 2] = ns.transpose(2, 3, 0, 5, 4, 1)
    return (y_prompt, y_sample, new_state)
```
